# Optimizing a Trainium2 kernel written in Bass

```python
import math, functools
import jax, jax.numpy as jnp
from jax import lax
import numpy as np

D_MODEL = 1024
BATCH = 16
SEQ = 2048
DEPTH = 1

D_MIX = D_MODEL
ATTN_WIDTH = D_MIX // 2
ATTN_HEADS = 8
ATTN_HEAD_DIM = ATTN_WIDTH // ATTN_HEADS
MLSTM_WIDTH = D_MIX - ATTN_WIDTH
MLSTM_HEADS = 4
MLSTM_HEAD_DIM = MLSTM_WIDTH // MLSTM_HEADS
DILATED_CONFIGS = ((128, 1), (512, 4), (2048, 16))
ATTN_BLOCK = 128
MLSTM_CHUNK = 64
CONV_WIDTH = 4
D_FF = ((8 * D_MODEL // 3) + 127) // 128 * 128
NORM_EPS = 1e-6
D_IN = 3 * ATTN_WIDTH + 4 * MLSTM_WIDTH + 2 * MLSTM_HEADS
SPLIT_POINTS = (ATTN_WIDTH, 2 * ATTN_WIDTH, 3 * ATTN_WIDTH,
                3 * ATTN_WIDTH + MLSTM_WIDTH, 3 * ATTN_WIDTH + 2 * MLSTM_WIDTH,
                3 * ATTN_WIDTH + 3 * MLSTM_WIDTH, 3 * ATTN_WIDTH + 4 * MLSTM_WIDTH,
                3 * ATTN_WIDTH + 4 * MLSTM_WIDTH + MLSTM_HEADS)

kernel_name = "hybrid_dilated_attn_mlstm_macaron"


def rmsnorm(x, w):
    xf = x.astype(jnp.float32)
    y = xf * lax.rsqrt(jnp.mean(xf * xf, axis=-1, keepdims=True) + NORM_EPS)
    return (y * w.astype(jnp.float32)).astype(x.dtype)


def swiglu(x, w_gate, w_up, w_down):
    return (jax.nn.silu(x @ w_gate) * (x @ w_up)) @ w_down


def alibi_slopes(n_heads):
    h = np.arange(1, n_heads + 1, dtype=np.float32)
    return jnp.asarray(2.0 ** (-8.0 * h / n_heads), dtype=jnp.float32)


def split_heads(t, n_heads):
    B, S, _ = t.shape
    return t.reshape(B, S, n_heads, -1).transpose(0, 2, 1, 3)


def merge_heads(t):
    B, H, S, dh = t.shape
    return t.transpose(0, 2, 1, 3).reshape(B, S, H * dh)


def dilated_branch(q, k, v, slopes, window, dilation):
    B, H, S, hd = q.shape
    blk = ATTN_BLOCK
    n_steps = window // dilation
    assert n_steps <= blk
    L = S // dilation
    nb = -(-L // blk)
    Lp = nb * blk

    def residues(t):
        return t.reshape(B, H, L, dilation, hd).transpose(0, 1, 3, 2, 4)

    qb = jnp.pad(residues(q), ((0, 0), (0, 0), (0, 0), (0, Lp - L), (0, 0)))
    qb = qb.reshape(B, H, dilation, nb, blk, hd)

    def key_blocks(t):
        tp = jnp.pad(residues(t), ((0, 0), (0, 0), (0, 0), (blk, Lp - L), (0, 0)))
        tp = tp.reshape(B, H, dilation, nb + 1, blk, hd)
        return jnp.concatenate([tp[:, :, :, :-1], tp[:, :, :, 1:]], axis=4)

    kb, vb = key_blocks(k), key_blocks(v)
    steps = np.arange(blk)[:, None] + blk - np.arange(2 * blk)[None, :]
    band = (steps >= 0) & (steps <= n_steps)
    exists = (np.arange(nb)[:, None, None] > 0) | (np.arange(2 * blk)[None, None, :] >= blk)
    mask = jnp.asarray(band[None] & exists)
    dist = jnp.asarray((steps * dilation).astype(np.float32))
    s = jnp.einsum('bhrnqd,bhrnkd->bhrnqk', qb, kb)
    s = s - slopes.reshape(H, 1, 1, 1, 1) * dist
    s = jnp.where(mask, s, -jnp.inf)
    m = jnp.max(s, axis=-1, keepdims=True)
    p = jnp.exp(s - m)
    denom = jnp.sum(p, axis=-1, keepdims=True)
    o = jnp.einsum('bhrnqk,bhrnkd->bhrnqd', p, vb) / denom
    lse = (m + jnp.log(denom))[..., 0]
    o = o.reshape(B, H, dilation, Lp, hd)[:, :, :, :L].transpose(0, 1, 3, 2, 4).reshape(B, H, S, hd)
    lse = lse.reshape(B, H, dilation, Lp)[:, :, :, :L].transpose(0, 1, 3, 2).reshape(B, H, S)
    return o, lse


def dilated_attention(q, k, v):
    slopes = alibi_slopes(q.shape[1])
    outs, lses = [], []
    for window, dilation in DILATED_CONFIGS:
        o, lse = dilated_branch(q, k, v, slopes, window, dilation)
        outs.append(o)
        lses.append(lse)
    w = jax.nn.softmax(jnp.stack(lses, axis=0), axis=0)
    return jnp.einsum('gbhs,gbhsd->bhsd', w, jnp.stack(outs, axis=0))


def causal_dwconv(x, w, b):
    K = w.shape[0]
    S = x.shape[1]
    xp = jnp.pad(x, ((0, 0), (K - 1, 0), (0, 0)))
    y = b
    for j in range(K):
        y = y + xp[:, j:j + S] * w[j]
    return y


def mlstm_chunkwise(q, k, v, i_pre, f_pre):
    B, H, S, dh = q.shape
    L = MLSTM_CHUNK
    nc = S // L
    logf = jax.nn.log_sigmoid(f_pre)

    def chunks(t):
        t = t.reshape((B, H, nc, L) + t.shape[3:])
        return jnp.moveaxis(t, 2, 0)

    causal = jnp.asarray(np.tril(np.ones((L, L), dtype=bool)))

    def step(carry, inp):
        C, n, m = carry
        qc, kc, vc, ic, lfc = inp
        b = jnp.cumsum(lfc, axis=-1)
        D = b[..., :, None] - b[..., None, :] + ic[..., None, :]
        D = jnp.where(causal, D, -jnp.inf)
        g = b + m[..., None]
        m_row = jnp.maximum(g, jnp.max(D, axis=-1))
        Dw = jnp.exp(D - m_row[..., None])
        gw = jnp.exp(g - m_row)
        sc = jnp.einsum('bhtd,bhsd->bhts', qc, kc) * Dw
        num = gw[..., None] * jnp.einsum('bhtd,bhde->bhte', qc, C) + jnp.einsum('bhts,bhse->bhte', sc, vc)
        den = gw * jnp.einsum('bhtd,bhd->bht', qc, n) + jnp.sum(sc, axis=-1)
        h = num / jnp.maximum(jnp.abs(den), jnp.exp(-m_row))[..., None]
        bL = b[..., -1]
        a = bL[..., None] - b + ic
        m_new = jnp.maximum(bL + m, jnp.max(a, axis=-1))
        decay = jnp.exp(bL + m - m_new)
        w = jnp.exp(a - m_new[..., None])
        C_new = decay[..., None, None] * C + jnp.einsum('bhs,bhsd,bhse->bhde', w, kc, vc)
        n_new = decay[..., None] * n + jnp.einsum('bhs,bhsd->bhd', w, kc)
        return (C_new, n_new, m_new), h

    init = (jnp.zeros((B, H, dh, dh), jnp.float32), jnp.zeros((B, H, dh), jnp.float32),
            jnp.zeros((B, H), jnp.float32))
    _, hs = lax.scan(step, init, (chunks(q), chunks(k), chunks(v), chunks(i_pre), chunks(logf)))
    return jnp.moveaxis(hs, 0, 2).reshape(B, H, S, dh)


def head_rmsnorm(t, w):
    return t * lax.rsqrt(jnp.mean(t * t, axis=-1, keepdims=True) + NORM_EPS) * w


def hybrid_mixer(h, w_in, q_norm_w, k_norm_w, conv_w, conv_b, i_bias, f_bias,
                 attn_out_gain, mlstm_out_gain, w_out):
    dtype = h.dtype
    proj = (h @ w_in).astype(jnp.float32)
    qa, ka, va, qm, km, vm, om, ig, fg = jnp.split(proj, SPLIT_POINTS, axis=-1)

    scale = ATTN_HEAD_DIM ** -0.5
    qa = head_rmsnorm(split_heads(qa, ATTN_HEADS), q_norm_w.astype(jnp.float32)) * scale
    ka = head_rmsnorm(split_heads(ka, ATTN_HEADS), k_norm_w.astype(jnp.float32))
    va = split_heads(va, ATTN_HEADS)
    attn = dilated_attention(qa, ka, va)
    attn = head_rmsnorm(attn, attn_out_gain.astype(jnp.float32).reshape(ATTN_HEADS, 1, ATTN_HEAD_DIM))

    qk = jax.nn.silu(causal_dwconv(jnp.concatenate([qm, km], axis=-1),
                                   conv_w.astype(jnp.float32), conv_b.astype(jnp.float32)))
    qm, km = jnp.split(qk, 2, axis=-1)
    qm = split_heads(qm, MLSTM_HEADS)
    km = split_heads(km, MLSTM_HEADS) * (MLSTM_HEAD_DIM ** -0.5)
    vm = split_heads(vm, MLSTM_HEADS)
    i_pre = (ig + i_bias.astype(jnp.float32)).transpose(0, 2, 1)
    f_pre = (fg + f_bias.astype(jnp.float32)).transpose(0, 2, 1)
    hm = mlstm_chunkwise(qm, km, vm, i_pre, f_pre)
    hm = jax.nn.sigmoid(split_heads(om, MLSTM_HEADS)) * hm
    hm = head_rmsnorm(hm, mlstm_out_gain.astype(jnp.float32).reshape(MLSTM_HEADS, 1, MLSTM_HEAD_DIM))

    y = jnp.concatenate([merge_heads(attn), merge_heads(hm)], axis=-1).astype(dtype)
    return y @ w_out


def setup_inputs(seed: int = 0) -> dict:
    key = jax.random.key(seed)
    ks = jax.random.split(key, 24)
    f32 = jnp.float32

    def normal(k, shape, scale):
        return jax.random.normal(k, shape, f32) * scale

    def gain(k, shape):
        return 1.0 + 0.05 * jax.random.normal(k, shape, f32)

    f_bias = (jnp.linspace(3.0, 6.0, MLSTM_HEADS, dtype=f32)[None, :]
              + 0.1 * jax.random.normal(ks[12], (DEPTH, MLSTM_HEADS), f32))
    return {
        "x": jax.random.normal(ks[0], (BATCH, SEQ, D_MODEL), f32),
        "ffn1_norm_w": gain(ks[1], (DEPTH, D_MODEL)),
        "ffn1_w_gate": normal(ks[2], (DEPTH, D_MODEL, D_FF), D_MODEL ** -0.5),
        "ffn1_w_up": normal(ks[3], (DEPTH, D_MODEL, D_FF), D_MODEL ** -0.5),
        "ffn1_w_down": normal(ks[4], (DEPTH, D_FF, D_MODEL), D_FF ** -0.5),
        "mix_norm_w": gain(ks[5], (DEPTH, D_MODEL)),
        "w_in": normal(ks[6], (DEPTH, D_MODEL, D_IN), D_MODEL ** -0.5),
        "q_norm_w": gain(ks[7], (DEPTH, ATTN_HEAD_DIM)),
        "k_norm_w": gain(ks[8], (DEPTH, ATTN_HEAD_DIM)),
        "conv_w": normal(ks[9], (DEPTH, CONV_WIDTH, 2 * MLSTM_WIDTH), CONV_WIDTH ** -0.5),
        "conv_b": normal(ks[10], (DEPTH, 2 * MLSTM_WIDTH), 0.02),
        "i_bias": normal(ks[11], (DEPTH, MLSTM_HEADS), 0.1),
        "f_bias": f_bias,
        "attn_out_gain": gain(ks[13], (DEPTH, ATTN_WIDTH)),
        "mlstm_out_gain": gain(ks[14], (DEPTH, MLSTM_WIDTH)),
        "w_out": normal(ks[15], (DEPTH, D_MIX, D_MODEL), D_MIX ** -0.5),
        "ffn2_norm_w": gain(ks[16], (DEPTH, D_MODEL)),
        "ffn2_w_gate": normal(ks[17], (DEPTH, D_MODEL, D_FF), D_MODEL ** -0.5),
        "ffn2_w_up": normal(ks[18], (DEPTH, D_MODEL, D_FF), D_MODEL ** -0.5),
        "ffn2_w_down": normal(ks[19], (DEPTH, D_FF, D_MODEL), D_FF ** -0.5),
    }


def reference(x, ffn1_norm_w, ffn1_w_gate, ffn1_w_up, ffn1_w_down, mix_norm_w, w_in,
              q_norm_w, k_norm_w, conv_w, conv_b, i_bias, f_bias, attn_out_gain,
              mlstm_out_gain, w_out, ffn2_norm_w, ffn2_w_gate, ffn2_w_up, ffn2_w_down):
    for l in range(DEPTH):
        x = x + 0.5 * swiglu(rmsnorm(x, ffn1_norm_w[l]), ffn1_w_gate[l], ffn1_w_up[l], ffn1_w_down[l])
        x = x + hybrid_mixer(rmsnorm(x, mix_norm_w[l]), w_in[l], q_norm_w[l], k_norm_w[l],
                             conv_w[l], conv_b[l], i_bias[l], f_bias[l], attn_out_gain[l],
                             mlstm_out_gain[l], w_out[l])
        x = x + 0.5 * swiglu(rmsnorm(x, ffn2_norm_w[l]), ffn2_w_gate[l], ffn2_w_up[l], ffn2_w_down[l])
    return x
```

```python
import math
from contextlib import ExitStack

import numpy as np
import concourse.bass as bass
import concourse.mybir as mybir
from concourse.bass_utils import run_bass_kernel_spmd

F32 = mybir.dt.float32
BF16 = mybir.dt.bfloat16
AF = mybir.ActivationFunctionType
ALU = mybir.AluOpType

NCORES = 8
S = 2048
D = 1024
DFF = 2816
NFC = DFF // 128
NKC = D // 128
NSP = S // 512
DIN = 3592
EPS = 1e-6


class Buf:
    __slots__ = ("name", "w", "r", "excl")

    def __init__(self, name, excl=False):
        self.name = name
        self.w = None
        self.r = []
        self.excl = excl


class Op:
    __slots__ = ("eng", "fn", "deps", "odeps", "is_dma", "slot", "signal", "sem", "val", "tag", "cost", "lat", "aset", "idx")

    def __init__(self, eng, fn, is_dma, slot):
        self.eng = eng
        self.fn = fn
        self.is_dma = is_dma
        self.slot = slot
        self.deps = []
        self.odeps = []
        self.cost = 100.0
        self.lat = 0.0
        self.aset = None
        self.signal = is_dma
        self.sem = None
        self.val = 0


class Sched:
    ENGS = ("pe", "act", "dve", "pool", "sp")

    def __init__(self, nc):
        self.nc = nc
        self.ops = []
        self.final_waits = []
        self.tag = ""

    def op(self, eng, fn, reads=(), writes=(), dma=False, slot=None, cost=100.0, lat=0.0, aset=None):
        o = Op(eng, fn, dma, slot)
        o.tag = self.tag
        o.cost = cost
        o.lat = lat
        o.aset = aset
        deps = {}
        for b in reads:
            if b.w is not None:
                deps[id(b.w)] = (b.w, True)
            if b.excl:
                for r in b.r:
                    if r.eng != eng and id(r) not in deps:
                        deps[id(r)] = (r, False)
        for b in writes:
            if b.w is not None and id(b.w) not in deps:
                deps[id(b.w)] = (b.w, False)
            for r in b.r:
                if id(r) not in deps:
                    deps[id(r)] = (r, False)
        for d, raw in deps.values():
            if d is o:
                continue
            if (not d.is_dma) and (not dma) and d.eng == eng and not raw:
                o.odeps.append(d)
                continue
            o.deps.append(d)
        for b in reads:
            b.r.append(o)
        for b in writes:
            b.w = o
            b.r = []
        self.ops.append(o)
        return o

    def schedule(self, W=32, sem_lat=120.0):
        for i, o in enumerate(self.ops):
            o.idx = i
        rem = {e: [o for o in self.ops if o.eng == e] for e in self.ENGS}
        if W <= 1:
            return rem
        order = {e: [] for e in self.ENGS}
        self.trace_sim = []
        tfree = {e: 0.0 for e in self.ENGS}
        fin = {}
        issued = set()
        cur_set = [None]
        total = len(self.ops)
        while total:
            best = None
            for e in self.ENGS:
                lst = rem[e]
                tf = tfree[e]
                for k in range(min(W, len(lst))):
                    o = lst[k]
                    r = tf
                    ok = True
                    for d in o.deps:
                        f = fin.get(id(d))
                        if f is None:
                            ok = False
                            break
                        if f + sem_lat > r:
                            r = f + sem_lat
                    if not ok:
                        continue
                    for d in o.odeps:
                        if id(d) not in issued:
                            ok = False
                            break
                    if not ok:
                        continue
                    if o.aset is not None and cur_set[0] is not None and o.aset != cur_set[0]:
                        r += 1300.0
                    if best is None or (r, o.idx) < best[0]:
                        best = ((r, o.idx), e, k, o)
                    if r <= tf:
                        break
            assert best is not None
            (r, _), e, k, o = best
            rem[e].pop(k)
            order[e].append(o)
            issued.add(id(o))
            if o.aset is not None:
                cur_set[0] = o.aset
            self.trace_sim.append((e, o.tag, r, o.cost, tfree[e]))
            tfree[e] = r + o.cost
            fin[id(o)] = r + o.cost + o.lat
            total -= 1
        self.sim_time = max(tfree.values())
        return order

    def emit(self, stack, W=64):
        nc = self.nc
        streams = self.schedule(W)
        for o in self.ops:
            for d in o.deps:
                d.signal = True
        for o in self.final_waits:
            o.signal = True
        eng_sem = {e: stack.enter_context(nc.semaphore("sem_" + e)) for e in self.ENGS}
        cnt = {e: 0 for e in self.ENGS}
        slot_sem = {}
        slot_cnt = {}
        for o in [o for e in self.ENGS for o in streams[e]]:
            if o.is_dma:
                if o.slot not in slot_sem:
                    slot_sem[o.slot] = stack.enter_context(nc.semaphore("dq_" + o.slot))
                    slot_cnt[o.slot] = 0
                slot_cnt[o.slot] += 16
                o.sem = slot_sem[o.slot]
                o.val = slot_cnt[o.slot]
            elif o.signal:
                cnt[o.eng] += 1
                o.sem = eng_sem[o.eng]
                o.val = cnt[o.eng]
        self.n_sems = len(slot_sem) + len(eng_sem)
        finals = self.final_waits

        def run(e, eng):
            waited = {}
            for o in streams[e]:
                need = {}
                for d in o.deps:
                    k = id(d.sem)
                    if waited.get(k, 0) >= d.val:
                        continue
                    if k not in need or need[k][1] < d.val:
                        need[k] = (d.sem, d.val)
                for k, (sem, val) in need.items():
                    eng.wait_ge(sem, val)
                    waited[k] = val
                ins = o.fn(eng)
                if o.signal:
                    ins.then_inc(o.sem, 16 if o.is_dma else 1)
            if e == "sp":
                for o in finals:
                    if waited.get(id(o.sem), 0) < o.val:
                        eng.wait_ge(o.sem, o.val)
                        waited[id(o.sem)] = o.val

        with nc.Block() as block:
            @block.tensor
            def _(eng):
                run("pe", eng)

            @block.scalar
            def _(eng):
                run("act", eng)

            @block.vector
            def _(eng):
                run("dve", eng)

            @block.gpsimd
            def _(eng):
                run("pool", eng)

            @block.sync
            def _(eng):
                run("sp", eng)


GROUPS = (8, 7, 7)
NWD = 8
NWS = 3
ARENA = 55488
LN_QSCALE = math.log(0.125)
NMT = 8 * 5 * 128
C_MT = 0
C_ID = C_MT + NMT
C_BD = C_ID + 128
C_CA = C_BD + 128
C_ON = C_CA + 128
C_OM = C_ON + 128
C_WN = C_OM + 128
C_END = C_WN + 64


class Builder:
    def __init__(self, stage="full", nseq=2):
        self.stage = stage
        self.nseq = nseq
        self.nc = bass.Bass("TRN2", target_bir_lowering=False)
        self.stack = ExitStack()
        self.sc = Sched(self.nc)
        self.cnt = {}

    def sb(self, name, shape, dt):
        return self.stack.enter_context(self.nc.sbuf_tensor(name, list(shape), dt))

    def ps(self, name, shape, dt=F32):
        return self.stack.enter_context(self.nc.psum_tensor(name, list(shape), dt))

    def dram(self, name, shape, dt, kind="ExternalInput"):
        return self.nc.dram_tensor(name, list(shape), dt, kind=kind).ap()

    def buf(self, name):
        return Buf(name)

    def bufs(self, name, n):
        return [Buf(f"{name}{i}") for i in range(n)]

    def rr(self, key, n):
        v = self.cnt.get(key, 0)
        self.cnt[key] = v + 1
        return v % n

    def carve(self, nbytes):
        o = self.ar_off
        assert o % 4 == 0
        self.ar_off += (nbytes + 3) // 4 * 4
        assert self.ar_off <= ARENA, (self.ar_off, ARENA)
        return self.AR[:, o // 2:(o + nbytes) // 2]


    @staticmethod
    def fsz(ap):
        n = 1
        for d in ap.shape[1:]:
            n *= d
        return n

    def mm(self, out, lhsT, rhs, reads, writes, start=True, stop=True, skip=False):
        n = self.fsz(rhs)
        c = max(64, n) / 2.2 + 25.0
        if rhs.dtype == F32:
            c *= 4
        return self.sc.op("pe", lambda e: e.matmul(out, lhsT, rhs, start=start, stop=stop,
                                                   skip_group_check=skip), reads=reads, writes=writes, cost=c, lat=60.0)

    ASETS = {AF.Silu: "silu", AF.Sigmoid: "sig", AF.Exp: "lnexp", AF.Ln: "lnexp"}

    def act(self, out, in_, func, reads, writes, **kw):
        c = 230.0 + 0.84 * self.fsz(in_)
        return self.sc.op("act", lambda e: e.activation(out=out, in_=in_, func=func, **kw),
                          reads=reads, writes=writes, cost=c, lat=60.0, aset=self.ASETS.get(func))

    def stt(self, out, in0, scalar, in1, op0, op1, reads, writes):
        c = 120.0 + 1.05 * self.fsz(in0)
        return self.sc.op("dve", lambda e: e.scalar_tensor_tensor(out=out, in0=in0, scalar=scalar, in1=in1,
                                                                  op0=op0, op1=op1), reads=reads, writes=writes,
                          cost=c, lat=60.0)

    def tt(self, out, in0, in1, op, reads, writes, eng="dve"):
        n = self.fsz(in0)
        c = 120.0 + (0.55 * n if (in0.dtype == BF16 and in1.dtype == BF16) else 1.05 * n)
        if eng == "pool":
            c = 200.0 + 2.0 * n
        return self.sc.op(eng, lambda e: e.tensor_tensor(out=out, in0=in0, in1=in1, op=op),
                          reads=reads, writes=writes, cost=c, lat=60.0)

    def ts(self, out, in0, s1, s2, op0, op1, reads, writes, eng="dve"):
        c = 120.0 + 0.6 * self.fsz(in0)
        if op1 is None:
            return self.sc.op(eng, lambda e: e.tensor_scalar(out=out, in0=in0, scalar1=s1, scalar2=None, op0=op0),
                              reads=reads, writes=writes, cost=c, lat=60.0)
        return self.sc.op(eng, lambda e: e.tensor_scalar(out=out, in0=in0, scalar1=s1, scalar2=s2, op0=op0, op1=op1),
                          reads=reads, writes=writes, cost=c, lat=60.0)

    def cp(self, out, in_, reads, writes, eng="dve"):
        c = 120.0 + 1.05 * self.fsz(in_)
        return self.sc.op(eng, lambda e: e.tensor_copy(out=out, in_=in_), reads=reads, writes=writes, cost=c, lat=60.0)

    def dma(self, eng, out, in_, reads, writes, slot):
        nbytes = 128 * self.fsz(out) * 4
        return self.sc.op(eng, lambda e: e.dma_start(out=out, in_=in_), reads=reads, writes=writes, dma=True, slot=slot,
                          cost=600.0, lat=2500.0 + nbytes / 120.0)

    def build(self):
        nc, sc = self.nc, self.sc
        NS = self.nseq
        self.xT = self.dram("xT", [NS, NKC, 128, S], F32)
        self.outT = self.dram("outT", [NS, NKC, 128, S], F32, "ExternalOutput")
        self.normw = self.dram("normw", [128, 3 * NKC], F32)
        self.wgu = [self.dram(f"wgu{f}", [NFC, 128, 2048], F32) for f in range(2)]
        self.wd = [self.dram(f"wd{f}", [NFC, 128, D], F32) for f in range(2)]
        self.wqk_a = self.dram("wqk_a", [4, 128, 2048], F32)
        self.wv_a = self.dram("wv_a", [4, 128, 1024], F32)
        self.wqk_m = self.dram("wqk_m", [4, 128, 2048], F32)
        self.wv_m = self.dram("wv_m", [4, 128, 1024], F32)
        self.wo_m = self.dram("wo_m", [4, 128, 1024], F32)
        self.wg_d = self.dram("wg", [128, 64], F32)
        self.wout = self.dram("wout", [8, 128, 1024], F32)
        self.small_d = self.dram("small", [128, 80], F32)
        self.ctab_d = self.dram("ctab", [128, C_END], F32)
        self.ctab32_d = self.dram("ctab32", [128, 256], F32)

        self.XR = self.sb("XR", [128, NKC, S], F32)
        self.HT = self.sb("HT", [128, NKC, S], BF16)
        self.XRb = [self.bufs(f"XR{k}_", NSP) for k in range(NKC)]
        self.HTb = [self.bufs(f"HT{k}_", NSP) for k in range(NKC)]
        self.NW = self.sb("NW", [128, 3 * NKC], F32)
        self.SM = self.sb("SM", [128, 80], F32)
        self.CT = self.sb("CT", [128, C_END], BF16)
        self.CT32 = self.sb("CT32", [128, 256], F32)
        self.WG = self.sb("WG", [128, 64], BF16)
        self.ones_mean = self.sb("ones_mean", [128, 128], BF16)
        self.epsb = self.sb("epsb", [128, 1], F32)
        self.lnq = self.sb("lnq", [128, 1], F32)
        self.dummy = self.sb("dummyt", [128, 2], F32)
        self.joint = self.sb("joint", [128, 2], F32)
        self.constb = self.buf("consts")
        self.WGUs = self.sb("WGUs", [128, NWS, 2048], BF16)
        self.WGUb = self.bufs("WGU", NWS)
        self.WDs = self.sb("WDs", [128, NWD, D], BF16)
        self.WDb = self.bufs("WD", NWD)
        self.wgu_cnt = 0
        self.wd_cnt = 0
        self.SQ = self.sb("SQ", [128, 3, 512], BF16)
        self.SQb = self.bufs("SQ", 3)
        self.T32 = self.sb("T32", [128, 2, 512], F32)
        self.T32b = self.bufs("T32", 2)
        self.RS = self.sb("RS", [128, 2, 512], F32)
        self.RSb = self.bufs("RS", 2)
        self.SG = self.sb("SG", [128, 2, 512], F32)
        self.SGb = self.bufs("SG", 2)
        self.AR = self.sb("AR", [128, ARENA // 2], BF16)
        self.phase = self.buf("phase")
        self.PS = [self.ps(f"PS{i}", [128, 512]) for i in range(8)]
        self.PSb = [Buf(f"PS{i}", excl=True) for i in range(8)]

        self.ar_off = 0
        GM = max(GROUPS)
        self.ACTB = self.carve(GM * S * 2).rearrange("p (c t) -> p c t", c=GM)
        self.ACTb = [self.bufs(f"ACT{c}_", NSP) for c in range(GM)]
        self.ar_off = 0
        self.QT = self.carve(S * 2)
        self.KT = self.carve(S * 2)
        self.QTb = self.bufs("QT", NSP)
        self.KTb = self.bufs("KT", NSP)
        self.V = [self.carve(16 * 130 * 2).rearrange("p (m h c) -> p m h c", m=16, h=2) for _ in range(3)]
        self.Vb = [self.bufs(f"V{o}_", 16) for o in range(3)]
        self.NE = 6
        self.E = self.carve(self.NE * 512 * 2).rearrange("p (e t) -> p e t", e=self.NE)
        self.Eb = self.bufs("E", self.NE)
        attn_end = self.ar_off
        self.YTc = self.carve(4 * S * 2).rearrange("p (y t) -> p y t", y=4)
        self.YTb = [self.bufs(f"YT{y}_", NSP) for y in range(4)]
        self.mix_off = self.ar_off
        self.VT = self.carve(S * 2)
        self.VTb = self.bufs("VT", NSP)
        self.ar_off = 0
        self.XC = self.carve(2052 * 2)
        self.XCb = self.bufs("XC", NSP)
        self.QM = self.carve(S * 2)
        self.KM = self.carve(S * 2)
        self.QMb = self.bufs("QM", NSP)
        self.KMb = self.bufs("KM", NSP)
        self.DG = self.carve(2 * 4 * 128 * 2).rearrange("p (j t m) -> p j t m", j=2, t=4)
        self.DGb = self.bufs("DG", 2)
        self.VM = self.carve(16 * 128 * 2).rearrange("p (b m) -> p b m", b=16)
        self.VMb = self.bufs("VM", 16)
        self.OT = self.carve(S * 2)
        self.OTb = self.bufs("OT", NSP)
        self.PP = self.carve(2 * 128 * 2).rearrange("p (r m) -> p r m", r=2)
        self.PPb = self.bufs("PP", 2)
        self.KK = self.carve(2 * 128 * 2).rearrange("p (r m) -> p r m", r=2)
        self.KKb = self.bufs("KK", 2)
        self.CSf = self.carve(2 * 256 * 4).bitcast(F32).rearrange("p (r m) -> p r m", r=2)
        self.CSb = self.bufs("CS", 2)
        self.GT = self.carve(16 * 8 * 4).bitcast(F32).rearrange("p (b g) -> p b g", b=16)
        self.GTb = self.buf("GT")
        self.NLF = self.carve(16 * 4).bitcast(F32)
        self.NEGB = self.carve(16 * 4).bitcast(F32)
        self.EK = self.carve(16 * 4).bitcast(F32)
        self.EBL = self.carve(16 * 4).bitcast(F32)
        self.TMPA = self.carve(16 * 4).bitcast(F32)
        self.gb = {k: self.buf(k) for k in ("NLF", "NEGB", "EK", "TMPA")}
        self.EBLb = self.bufs("EBL", NSP)
        assert self.ar_off <= attn_end, (self.ar_off, attn_end)
        self.ar_off = self.mix_off
        self.RR = self.carve(2 * 512 * 4).bitcast(F32).rearrange("p (r m) -> p r m", r=2)
        self.RRb = self.bufs("RR", 2)
        self.ENB = self.carve(2 * 512 * 4).bitcast(F32).rearrange("p (r m) -> p r m", r=2)
        self.ENBb = self.bufs("ENB", 2)
        self.CB = self.carve(8 * 256 * 2).rearrange("p (c m) -> p c m", c=8)
        self.CBb = self.bufs("CB", 8)
        self.P3s = [self.PSb[3]] * 2
        self.P3t = [self.PSb[5]] * 2
        self.P4d = [self.PSb[4]] * 2
        self.PS3T = self.PS[5][:, 256:384].bitcast(BF16)
        self.fbn = self.sb("fbn", [128, 4], F32)
        self.ibk = self.sb("ibk", [128, 4], F32)
        self.oneb = self.sb("oneb", [128, 1], F32)
        self.lneps = self.sb("lneps", [128, 1], F32)

        self.msb = self.buf("ms")
        self.NWb = self.buf("NWb")
        for ap_, v in ((self.ones_mean[:], 1.0 / 1024.0), (self.epsb[:], EPS), (self.lnq[:], LN_QSCALE),
                       (self.oneb[:], 1.0), (self.lneps[:], math.log(EPS))):
            sc.op("pool", lambda e, ap_=ap_, v=v: e.memset(ap_, v), writes=[self.msb])
        self.dma("sp", self.NW[:], self.normw, [], [self.NWb], "c0")
        self.tables_done = False
        self.gate_consts_done = False

        full = self.stage in ("full", "ffn12")
        self.load_x(0)
        self.rmsnorm(0)
        for s in range(NS):
            if self.stage == "ffn1":
                self.ffn(s, 0)
            elif self.stage == "ffn12":
                self.ffn(s, 0, after_span=lambda n: self.rmsnorm_span(2, n))
            else:
                self.ffn(s, 0, after_span=lambda n: self.rmsnorm_span(1, n))
                self.mixer(s, after_span=(lambda n: self.rmsnorm_span(2, n)) if full else None)
            if full:
                def tail(n, s=s):
                    self.store_out(s, [n])
                    if s + 1 < NS:
                        self.load_x(s + 1, [n])
                        self.rmsnorm_span(0, n)
                self.ffn(s, 1, after_span=tail)
            else:
                self.store_out(s)
                if s + 1 < NS:
                    self.load_x(s + 1)
                    self.rmsnorm(0)
        sc.emit(self.stack)
        return nc

    def load_tables(self):
        sc = self.sc
        if self.tables_done:
            return
        self.tables_done = True
        tmpb = self.bufs("ctmp", 8)
        self.dma("sp", self.SM[:], self.small_d, [], [tmpb[0]], "c1")
        self.dma("sp", self.CT32[:], self.ctab32_d, [], [tmpb[1]], "c2")
        self.dma("pool", self.WG[:], self.wg_d, [], [tmpb[2]], "c3")
        for i, c0 in enumerate(range(0, C_END, 2048)):
            c1 = min(C_END, c0 + 2048)
            self.dma("pool", self.CT[:, c0:c1], self.ctab_d[:, c0:c1], [], [tmpb[3 + i]], f"ct{i}")
        sc.op("pool", lambda e: e.memset(self.joint[:], 0.0), reads=tmpb + [self.msb], writes=[self.constb])

    def barrier(self):
        self.sc.op("pool", lambda e: e.memset(self.dummy[:], 0.0), writes=[self.phase])

    def load_x(self, s, spans=range(NSP)):
        sc = self.sc
        for n in spans:
            for k in range(NKC):
                self.dma("sp", self.XR[:, k, n * 512:(n + 1) * 512], self.xT[s, k, :, n * 512:(n + 1) * 512],
                         [], [self.XRb[k][n]], f"x{k}_{n}")

    def store_out(self, s, spans=range(NSP)):
        sc = self.sc
        for n in spans:
            for k in range(NKC):
                o = self.dma("sp", self.outT[s, k, :, n * 512:(n + 1) * 512], self.XR[:, k, n * 512:(n + 1) * 512],
                             [self.XRb[k][n]], [], f"o{k}_{n}")
                sc.final_waits.append(o)

    def rmsnorm(self, widx):
        for n in range(NSP):
            self.rmsnorm_span(widx, n)

    def rmsnorm_span(self, widx, n):
        sc = self.sc
        sc.tag = "norm"
        if True:
            sl = slice(n * 512, (n + 1) * 512)
            pi = 6 + self.rr("nps", 2)
            for k in range(NKC):
                qi = self.rr("sq", 3)
                if k % 2 == 0:
                    self.act(self.SQ[:, qi, :], self.XR[:, k, sl], AF.Square, reads=[self.XRb[k][n]],
                             writes=[self.SQb[qi]])
                else:
                    self.tt(self.SQ[:, qi, :], self.XR[:, k, sl], self.XR[:, k, sl], ALU.mult,
                            reads=[self.XRb[k][n]], writes=[self.SQb[qi]])
                self.mm(self.PS[pi][:], self.ones_mean[:], self.SQ[:, qi, :], reads=[self.SQb[qi], self.msb],
                        writes=[self.PSb[pi]], start=(k == 0), stop=(k == NKC - 1))
            ti = self.rr("t32", 2)
            ri = self.rr("rs", 2)
            self.act(self.T32[:, ti, :], self.PS[pi][:], AF.Ln, reads=[self.PSb[pi], self.msb],
                     writes=[self.T32b[ti]], bias=self.epsb[:])
            self.act(self.RS[:, ri, :], self.T32[:, ti, :], AF.Exp, reads=[self.T32b[ti]], writes=[self.RSb[ri]],
                     scale=-0.5)
            for k in range(NKC):
                self.stt(self.HT[:, k, sl], self.XR[:, k, sl], self.NW[:, widx * NKC + k:widx * NKC + k + 1],
                         self.RS[:, ri, :], ALU.mult, ALU.mult,
                         reads=[self.XRb[k][n], self.RSb[ri], self.NWb], writes=[self.HTb[k][n]])

    def load_w2048(self, src):
        si = self.wgu_cnt % NWS
        self.wgu_cnt += 1
        self.dma("pool", self.WGUs[:, si, :], src, [], [self.WGUb[si]], f"wgu{si}")
        return si

    def load_w1024(self, src, ci=None):
        if ci is None:
            ci = self.wd_cnt % NWD
            self.wd_cnt += 1
        self.dma("pool", self.WDs[:, ci, :], src, [], [self.WDb[ci]], f"wd{ci}")
        return ci

    def ffn(self, s, f, after_span=None):
        sc = self.sc
        self.barrier()
        c0 = 0
        PRE = NWS - 1
        slots = {}
        for c in range(PRE):
            slots[c] = self.load_w2048(self.wgu[f][c])
        self.load_tables()
        for GS in GROUPS:
            chunks = list(range(c0, c0 + GS))
            c0 += GS
            for ci, c in enumerate(chunks):
                if c + PRE < NFC:
                    slots[c + PRE] = self.load_w2048(self.wgu[f][c + PRE])
                self.load_w1024(self.wd[f][c], ci)
                si = slots[c]
                sc.tag = "ffn.gu"
                for n in range(NSP):
                    sl = slice(n * 512, (n + 1) * 512)
                    pg = self.rr("pg", 2)
                    pu = 2 + self.rr("pu", 2)
                    for j, pi in ((0, pg), (1, pu)):
                        for k in range(NKC):
                            off = (j * NKC + k) * 128
                            self.mm(self.PS[pi][:], self.WGUs[:, si, off:off + 128], self.HT[:, k, sl],
                                    reads=[self.WGUb[si], self.HTb[k][n]], writes=[self.PSb[pi]],
                                    start=(k == 0), stop=(k == NKC - 1))
                    gi = self.rr("sg", 2)
                    self.act(self.SG[:, gi, :], self.PS[pg][:], AF.Silu, reads=[self.PSb[pg]], writes=[self.SGb[gi]])
                    self.tt(self.ACTB[:, ci, sl], self.PS[pu][:], self.SG[:, gi, :], ALU.mult,
                            reads=[self.PSb[pu], self.SGb[gi], self.phase], writes=[self.ACTb[ci][n]])
            last = (c0 == NFC)
            for n in range(NSP):
                sc.tag = "ffn.down"
                sl = slice(n * 512, (n + 1) * 512)
                for dc in range(NKC):
                    pd = 4 + self.rr("pd", 2)
                    for ci in range(GS):
                        self.mm(self.PS[pd][:], self.WDs[:, ci, dc * 128:(dc + 1) * 128], self.ACTB[:, ci, sl],
                                reads=[self.WDb[ci], self.ACTb[ci][n], self.phase], writes=[self.PSb[pd]],
                                start=(ci == 0), stop=(ci == GS - 1))
                    self.stt(self.XR[:, dc, sl], self.PS[pd][:], 0.5, self.XR[:, dc, sl], ALU.mult, ALU.add,
                             reads=[self.PSb[pd], self.XRb[dc][n]], writes=[self.XRb[dc][n]])
                if last and after_span is not None:
                    after_span(n)

    def mixer(self, s, after_span=None):
        sc = self.sc
        if not self.gate_consts_done:
            self.gate_consts_done = True
            self.ts(self.fbn[:], self.SM[:, 24:28], -1.0, None, ALU.mult, None, reads=[self.constb], writes=[self.constb])
            self.ts(self.ibk[:], self.SM[:, 20:24], -0.5 * math.log(128.0), None, ALU.add, None,
                    reads=[self.constb], writes=[self.constb])
        self.barrier()
        ph = [self.phase]
        for o in range(3):
            ap = self.V[o][:, :, :, 64:65]
            sc.op("pool", lambda e, ap=ap: e.memset(ap, 1.0), reads=ph, writes=self.Vb[o])
        for hp in range(4):
            self.attn_pair(s, hp)
        self.finish_group(0)
        if self.stage in ("full", "mix", "yT"):
            self.barrier()
            self.mlstm_gates()
            for hm in range(4):
                self.mlstm_head(s, hm)
            self.finish_group(1, after_span)

    def finish_group(self, g, after_span=None):
        if self.stage == "yT":
            for yi in range(4):
                for n in range(NSP):
                    sl = slice(n * 512, (n + 1) * 512)
                    self.cp(self.XR[:, 4 * g + yi, sl], self.YTc[:, yi, sl],
                            reads=[self.YTb[yi][n], self.phase], writes=[self.XRb[4 * g + yi][n]])
            return
        self.sc.tag = "wout"
        ois = [self.load_w1024(self.wout[4 * g + yi]) for yi in range(4)]
        for n in range(NSP):
            sl = slice(n * 512, (n + 1) * 512)
            for dc in range(NKC):
                pd = self.rr("pproj", 2)
                for yi in range(4):
                    self.mm(self.PS[pd][:], self.WDs[:, ois[yi], dc * 128:(dc + 1) * 128], self.YTc[:, yi, sl],
                            reads=[self.WDb[ois[yi]], self.YTb[yi][n], self.phase], writes=[self.PSb[pd]],
                            start=(yi == 0), stop=(yi == 3))
                self.tt(self.XR[:, dc, sl], self.PS[pd][:], self.XR[:, dc, sl], ALU.add,
                        reads=[self.PSb[pd], self.XRb[dc][n]], writes=[self.XRb[dc][n]])
            if after_span is not None:
                self.sc.tag = "norm"
                after_span(n)
                self.sc.tag = "wout"

    def attn_pair(self, s, hp):
        ph = [self.phase]
        CT = self.CT
        si = self.load_w2048(self.wqk_a[hp])
        vi = self.load_w1024(self.wv_a[hp])
        yi = hp
        self.sc.tag = "attn.qk"
        for j, dst, dstb in ((0, self.QT, self.QTb), (1, self.KT, self.KTb)):
            for n in range(NSP):
                sl = slice(n * 512, (n + 1) * 512)
                pa = (0, 1, 3, 4)[self.rr("pproj4", 4)]
                for k in range(NKC):
                    off = (j * NKC + k) * 128
                    self.mm(self.PS[pa][:], self.WGUs[:, si, off:off + 128], self.HT[:, k, sl],
                            reads=[self.WGUb[si], self.HTb[k][n]], writes=[self.PSb[pa]],
                            start=(k == 0), stop=(k == NKC - 1))
                qi = self.rr("sq", 3)
                self.act(self.SQ[:, qi, :], self.PS[pa][:], AF.Square, reads=[self.PSb[pa]], writes=[self.SQb[qi]])
                pb = 2
                self.mm(self.PS[pb][:], CT[:, C_BD:C_BD + 128], self.SQ[:, qi, :],
                        reads=[self.SQb[qi], self.constb], writes=[self.PSb[pb]])
                ti = self.rr("t32", 2)
                ri = self.rr("rs", 2)
                self.act(self.T32[:, ti, :], self.PS[pb][:], AF.Ln, reads=[self.PSb[pb], self.constb],
                         writes=[self.T32b[ti]], bias=self.epsb[:])
                if j == 0:
                    self.act(self.RS[:, ri, :], self.T32[:, ti, :], AF.Exp, reads=[self.T32b[ti], self.constb],
                             writes=[self.RSb[ri]], scale=-0.5, bias=self.lnq[:])
                else:
                    self.act(self.RS[:, ri, :], self.T32[:, ti, :], AF.Exp, reads=[self.T32b[ti]],
                             writes=[self.RSb[ri]], scale=-0.5)
                self.stt(dst[:, sl], self.PS[pa][:], self.SM[:, j:j + 1], self.RS[:, ri, :], ALU.mult, ALU.mult,
                         reads=[self.PSb[pa], self.RSb[ri], self.constb] + ph, writes=[dstb[n]])

        self.sc.tag = "attn.v"
        def tokcols(o, m):
            if o == 0:
                return slice(128 * m, 128 * m + 128), [m // 4]
            if o == 1:
                n, r = divmod(m, 4)
                return slice(512 * n + r, 512 * n + 512, 4), [n]
            return slice(m, S, 16), [0, 1, 2, 3]

        ident = CT[:, C_ID:C_ID + 128]
        for n in range(NSP):
            sl = slice(n * 512, (n + 1) * 512)
            pa = (0, 1, 3, 4)[self.rr("pproj4", 4)]
            for k in range(NKC):
                self.mm(self.PS[pa][:], self.WDs[:, vi, k * 128:(k + 1) * 128], self.HT[:, k, sl],
                        reads=[self.WDb[vi], self.HTb[k][n]], writes=[self.PSb[pa]],
                        start=(k == 0), stop=(k == NKC - 1))
            self.act(self.VT[:, sl], self.PS[pa][:], AF.Copy, reads=[self.PSb[pa]] + ph, writes=[self.VTb[n]])
        for o in range(3):
            for m0 in range(0, 16, 4):
                pv = self.rr("pproj", 2)
                pvb = self.PS[pv][:, 0:256].bitcast(BF16)
                for mm_ in range(4):
                    cols, spans = tokcols(o, m0 + mm_)
                    outap = pvb[:, mm_ * 128:(mm_ + 1) * 128]
                    inap = self.VT[:, cols]
                    self.sc.op("pe", lambda e, outap=outap, inap=inap: e.transpose(outap, inap, ident),
                               reads=[self.VTb[n] for n in spans] + [self.constb] + ph, writes=[self.PSb[pv]],
                               cost=90.0, lat=60.0)
                self.act(self.V[o][:, m0:m0 + 4, :, 0:64],
                         pvb.rearrange("p (a b c) -> p a b c", a=4, b=2, c=64), AF.Copy,
                         reads=[self.PSb[pv]] + ph, writes=[self.Vb[o][m0 + i] for i in range(4)])

        for hh in range(2):
            h = 2 * hp + hh
            rows = slice(hh * 64, hh * 64 + 64)
            for n in range(NSP):
                self.attn_unit(h, hh, rows, n, yi)

    def attn_unit(self, h, hh, rows, n, yi):
        ph = [self.phase]
        CT = self.CT
        sl = slice(n * 512, (n + 1) * 512)
        acc = 6 + self.rr("acc", 2)
        state = {"first": True}

        def mt(t):
            return C_MT + (h * 5 + t) * 128

        def pv_mm(vo, blk, ei, ecols, ocols):
            st = state["first"]
            state["first"] = False
            self.mm(self.PS[acc][0:65, ocols], self.V[vo][:, blk, hh, 0:65], self.E[:, ei, ecols],
                    reads=[self.Vb[vo][blk], self.Eb[ei]] + ph, writes=[self.PSb[acc]],
                    start=st, stop=False, skip=True)

        def exp_mask(pi, ei, c0, mask_ap, a):
            self.act(self.E[:, ei, c0:512], self.PS[pi][:, c0:512], AF.Exp,
                     reads=[self.PSb[pi]] + ph, writes=[self.Eb[ei]])
            ev = self.E[:, ei, c0:512].rearrange("p (a b) -> p a b", a=a)
            self.tt(ev, ev, mask_ap, ALU.mult, reads=[self.Eb[ei], self.constb] + ph, writes=[self.Eb[ei]])

        for o, tcur, tprev in ((0, 0, 1), (1, 2, 3)):
            for prev in (0, 1):
                self.sc.tag = "attn.d1" if o == 0 else "attn.d4"
                if o == 1 and prev and n == 0:
                    continue
                j0 = 1 if (o == 0 and prev and n == 0) else 0
                pi = 3 + self.rr("psc", 3)
                ei = self.rr("e", self.NE)
                kbs = {}
                for jq in range(j0, 4):
                    if o == 0:
                        qb = 4 * n + jq
                        kb = qb - prev
                        qc = slice(128 * qb, 128 * qb + 128)
                        kc_ = slice(128 * kb, 128 * kb + 128)
                        kspan = kb // 4
                    else:
                        kn = n - prev
                        kb = 4 * kn + jq
                        qc = slice(512 * n + jq, 512 * n + 512, 4)
                        kc_ = slice(512 * kn + jq, 512 * kn + 512, 4)
                        kspan = kn
                    kbs[jq] = kb
                    self.mm(self.PS[pi][:, jq * 128:(jq + 1) * 128], self.KT[rows, kc_], self.QT[rows, qc],
                            reads=[self.KTb[kspan], self.QTb[n]] + ph, writes=[self.PSb[pi]])
                t = tprev if prev else tcur
                nb = 4 - j0
                mask_ap = CT[:, mt(t):mt(t) + 128].unsqueeze(1).to_broadcast([128, nb, 128])
                exp_mask(pi, ei, j0 * 128, mask_ap, nb)
                for jq in range(j0, 4):
                    ocols = slice(jq * 128, jq * 128 + 128) if o == 0 else slice(jq, 512, 4)
                    pv_mm(o, kbs[jq], ei, slice(jq * 128, jq * 128 + 128), ocols)
        self.sc.tag = "attn.d16"
        pi = 3 + self.rr("psc", 3)
        ei = self.rr("e", self.NE)
        for r in range(16):
            self.mm(self.PS[pi][:, r * 32:(r + 1) * 32], self.KT[rows, slice(r, S, 16)],
                    self.QT[rows, slice(512 * n + r, 512 * n + 512, 16)],
                    reads=self.KTb + [self.QTb[n]] + ph, writes=[self.PSb[pi]])
        mask_ap = CT[:, mt(4) + 32 * n:mt(4) + 32 * n + 32].unsqueeze(1).to_broadcast([128, 16, 32])
        exp_mask(pi, ei, 0, mask_ap, 16)
        for r in range(16):
            pv_mm(2, r, ei, slice(r * 32, r * 32 + 32), slice(r, 512, 16))
        self.sc.tag = "attn.post"
        qi = self.rr("sq", 3)
        self.act(self.SQ[0:65, qi, :], self.PS[acc][0:65, :], AF.Square, reads=[self.PSb[acc]], writes=[self.SQb[qi]])
        pb = 2
        self.mm(self.PS[pb][0:64, :], CT[0:65, C_WN:C_WN + 64], self.SQ[0:65, qi, :],
                reads=[self.SQb[qi], self.constb], writes=[self.PSb[pb]])
        ti = self.rr("t32", 2)
        ri = self.rr("rs", 2)
        self.act(self.T32[0:64, ti, :], self.PS[pb][0:64, :], AF.Ln, reads=[self.PSb[pb]], writes=[self.T32b[ti]])
        self.act(self.RS[0:64, ri, :], self.T32[0:64, ti, :], AF.Exp, reads=[self.T32b[ti]], writes=[self.RSb[ri]],
                 scale=-0.5)
        self.stt(self.YTc[rows, yi, sl], self.PS[acc][0:64, :], self.SM[0:64, 8 + h:9 + h], self.RS[0:64, ri, :],
                 ALU.mult, ALU.mult, reads=[self.PSb[acc], self.RSb[ri], self.constb] + ph, writes=[self.YTb[yi][n]])

    def mlstm_gates(self):
        ph = [self.phase]
        self.sc.tag = "ml.gates"
        for blk in range(16):
            cols = slice(128 * blk, 128 * blk + 128)
            for k in range(NKC):
                self.mm(self.PS[5][:, blk * 8:(blk + 1) * 8], self.HT[:, k, cols], self.WG[:, k * 8:(k + 1) * 8],
                        reads=[self.HTb[k][blk // 4], self.constb] + ph, writes=[self.PSb[5]],
                        start=(k == 0), stop=(k == NKC - 1))
        self.cp(self.GT[:], self.PS[5][:, 0:128].rearrange("p (b g) -> p b g", b=16),
                reads=[self.PSb[5]] + ph, writes=[self.GTb])
        ap = self.XC[:, 0:3]
        self.sc.op("pool", lambda e: e.memset(ap, 0.0), reads=ph, writes=[self.XCb[0]])

    def mlstm_head(self, s, hm):
        ph = [self.phase]
        CT, CT32, gb = self.CT, self.CT32, self.gb
        si = self.load_w2048(self.wqk_m[hm])
        vi = self.load_w1024(self.wv_m[hm])
        oi = self.load_w1024(self.wo_m[hm])
        yi = hm
        ident = CT[:, C_ID:C_ID + 128]
        ones_bf = CT[:, C_ON:C_ON + 128]
        self.sc.tag = "ml.qkconv"
        for j, dst, dstb in ((0, self.QM, self.QMb), (1, self.KM, self.KMb)):
            c8 = j * 4 + hm
            wcol = lambda tap: self.SM[:, 36 + c8 * 4 + tap:37 + c8 * 4 + tap]
            for n in range(NSP):
                sl = slice(n * 512, (n + 1) * 512)
                pa = (0, 1, 3, 4)[self.rr("pproj4", 4)]
                for k in range(NKC):
                    off = (j * NKC + k) * 128
                    self.mm(self.PS[pa][:], self.WGUs[:, si, off:off + 128], self.HT[:, k, sl],
                            reads=[self.WGUb[si], self.HTb[k][n]], writes=[self.PSb[pa]],
                            start=(k == 0), stop=(k == NKC - 1))
                self.act(self.XC[:, 3 + 512 * n:3 + 512 * n + 512], self.PS[pa][:], AF.Copy,
                         reads=[self.PSb[pa]] + ph, writes=[self.XCb[n]])
                rd = [self.XCb[n]] + ([self.XCb[n - 1]] if n > 0 else [])
                gi = self.rr("sg", 2)
                self.ts(self.SG[:, gi, :], self.XC[:, 512 * n:512 * n + 512], wcol(0), self.SM[:, 28 + c8:29 + c8],
                        ALU.mult, ALU.add, reads=rd + [self.constb] + ph, writes=[self.SGb[gi]])
                for tap in (1, 2, 3):
                    self.stt(self.SG[:, gi, :], self.XC[:, 512 * n + tap:512 * n + tap + 512], wcol(tap),
                             self.SG[:, gi, :], ALU.mult, ALU.add, reads=rd + [self.SGb[gi], self.constb] + ph,
                             writes=[self.SGb[gi]])
                self.act(dst[:, sl], self.SG[:, gi, :], AF.Silu, reads=[self.SGb[gi]] + ph, writes=[dstb[n]])
        self.sc.tag = "ml.vo"
        for m0 in range(0, 16, 4):
            pv = self.rr("pproj", 2)
            for mm_ in range(4):
                blk = m0 + mm_
                cols = slice(128 * blk, 128 * blk + 128)
                for k in range(NKC):
                    self.mm(self.PS[pv][:, mm_ * 128:(mm_ + 1) * 128], self.HT[:, k, cols],
                            self.WDs[:, vi, k * 128:(k + 1) * 128],
                            reads=[self.WDb[vi], self.HTb[k][blk // 4]], writes=[self.PSb[pv]],
                            start=(k == 0), stop=(k == NKC - 1))
            self.act(self.VM[:, m0:m0 + 4, :], self.PS[pv][:, 0:512].rearrange("p (a b) -> p a b", a=4), AF.Copy,
                     reads=[self.PSb[pv]] + ph, writes=[self.VMb[m0 + i] for i in range(4)])
        for n in range(NSP):
            sl = slice(n * 512, (n + 1) * 512)
            pa = self.rr("pproj", 2)
            for k in range(NKC):
                self.mm(self.PS[pa][:], self.WDs[:, oi, k * 128:(k + 1) * 128], self.HT[:, k, sl],
                        reads=[self.WDb[oi], self.HTb[k][n]], writes=[self.PSb[pa]],
                        start=(k == 0), stop=(k == NKC - 1))
            self.act(self.OT[:, sl], self.PS[pa][:], AF.Sigmoid, reads=[self.PSb[pa]] + ph, writes=[self.OTb[n]])
        self.sc.tag = "ml.gate2"
        self.act(self.TMPA[:], self.GT[:, :, 4 + hm], AF.Exp, reads=[self.GTb, self.constb] + ph, writes=[gb["TMPA"]],
                 scale=-1.0, bias=self.fbn[:, hm:hm + 1])
        self.act(self.NLF[:], self.TMPA[:], AF.Ln, reads=[gb["TMPA"], self.constb] + ph, writes=[gb["NLF"]],
                 bias=self.oneb[:])
        self.mm(self.PS[5][:, 128:144], CT32[:, 0:128], self.NLF[:], reads=[gb["NLF"], self.constb] + ph,
                writes=[self.PSb[5]])
        self.tt(self.NEGB[:], self.PS[5][:, 128:144], self.GT[:, :, hm], ALU.add,
                reads=[self.PSb[5], self.GTb] + ph, writes=[gb["NEGB"]])
        self.act(self.EK[:], self.NEGB[:], AF.Exp, reads=[gb["NEGB"], self.constb] + ph, writes=[gb["EK"]],
                 bias=self.ibk[:, hm:hm + 1])

        def span_rows(n):
            self.sc.tag = "ml.rows"
            ri = self.rr("rr", 2)
            self.tt(self.RR[:, ri, :].rearrange("p (a b) -> p a b", a=4),
                    CT32[:, 0:128].unsqueeze(1).to_broadcast([128, 4, 128]),
                    self.NLF[:, 4 * n:4 * n + 4].unsqueeze(2).to_broadcast([128, 4, 128]), ALU.mult,
                    reads=[gb["NLF"], self.constb] + ph, writes=[self.RRb[ri]])
            self.mm(self.PS[5][:], CT32[:, 128:256], self.RR[:, ri, :], reads=[self.RRb[ri], self.constb] + ph,
                    writes=[self.PSb[5]])
            ei = self.rr("enb", 2)
            self.act(self.ENB[:, ei, :], self.PS[5][:], AF.Exp, reads=[self.PSb[5], self.constb] + ph,
                     writes=[self.ENBb[ei]], scale=2.0, bias=self.lneps[:])
            self.act(self.EBL[:, 4 * n:4 * n + 4], self.PS[5][:, 127:512:128], AF.Exp,
                     reads=[self.PSb[5]] + ph, writes=[self.EBLb[n]], scale=-1.0)
            return ei

        def stage0(c):
            self.sc.tag = "ml.s0"
            cols = slice(128 * c, 128 * c + 128)
            r = c % 2
            self.mm(self.PS[3][:, r * 128:(r + 1) * 128], self.KM[:, cols], self.QM[:, cols],
                    reads=[self.KMb[c // 4], self.QMb[c // 4]] + ph, writes=[self.P3s[r]])
            self.stt(self.PP[:, r, :], self.PS[3][:, r * 128:(r + 1) * 128], self.EK[:, c:c + 1],
                     CT[:, C_CA:C_CA + 128], ALU.mult, ALU.mult,
                     reads=[self.P3s[r], gb["EK"], self.constb] + ph, writes=[self.PPb[r]])
            tp = self.PS3T[:, r * 128:(r + 1) * 128]
            self.sc.op("pe", lambda e: e.transpose(tp, self.KM[:, cols], ident),
                       reads=[self.KMb[c // 4], self.constb] + ph, writes=[self.P3t[r]], cost=90.0, lat=60.0)
            self.act(self.KK[:, r, :], tp, AF.Copy, reads=[self.P3t[r], gb["EK"]] + ph, writes=[self.KKb[r]],
                     scale=self.EK[:, c:c + 1])

        def stage1(c):
            self.sc.tag = "ml.s1"
            r = c % 2
            d0 = r * 256
            self.mm(self.PS[4][:, d0:d0 + 128], self.KK[:, r, :], self.VM[:, c, :],
                    reads=[self.KKb[r], self.VMb[c]] + ph, writes=[self.P4d[r]])
            self.mm(self.PS[4][:, d0 + 128:d0 + 256], self.KK[:, r, :], ones_bf,
                    reads=[self.KKb[r], self.constb] + ph, writes=[self.P4d[r]])
            if c == 0:
                self.cp(self.CSf[:, r, :], self.PS[4][:, d0:d0 + 256], reads=[self.P4d[r]] + ph, writes=[self.CSb[r]])
            else:
                self.stt(self.CSf[:, r, :], self.CSf[:, 1 - r, :], self.EBL[:, c - 1:c], self.PS[4][:, d0:d0 + 256],
                         ALU.mult, ALU.add, reads=[self.CSb[1 - r], self.EBLb[(c - 1) // 4], self.P4d[r]] + ph,
                         writes=[self.CSb[r]])
            if c < 15:
                self.act(self.CB[:, c % 8, :], self.CSf[:, r, :], AF.Copy, reads=[self.CSb[r], self.EBLb[c // 4]] + ph,
                         writes=[self.CBb[c % 8]], scale=self.EBL[:, c:c + 1])

        def stage2(c, pn, pq):
            self.sc.tag = "ml.s2"
            cols = slice(128 * c, 128 * c + 128)
            r = c % 2
            cc = c % 4
            osl = slice(cc * 128, cc * 128 + 128)
            self.mm(self.PS[pn][:, osl], self.VM[:, c, :], self.PP[:, r, :],
                    reads=[self.VMb[c], self.PPb[r]] + ph, writes=[self.PSb[pn]], start=True, stop=(c == 0), skip=True)
            if c > 0:
                self.mm(self.PS[pn][:, osl], self.CB[:, (c - 1) % 8, 0:128], self.QM[:, cols],
                        reads=[self.CBb[(c - 1) % 8], self.QMb[c // 4]] + ph, writes=[self.PSb[pn]],
                        start=False, stop=True, skip=True)
            self.mm(self.PS[pq][:, osl], ones_bf, self.PP[:, r, :],
                    reads=[self.PPb[r], self.constb] + ph, writes=[self.PSb[pq]], start=True, stop=(c == 0), skip=True)
            if c > 0:
                self.mm(self.PS[pq][:, osl], self.CB[:, (c - 1) % 8, 128:256], self.QM[:, cols],
                        reads=[self.CBb[(c - 1) % 8], self.QMb[c // 4]] + ph, writes=[self.PSb[pq]],
                        start=False, stop=True, skip=True)

        def post(n, pn, pq, ei):
            self.sc.tag = "ml.post"
            sl = slice(n * 512, (n + 1) * 512)
            gi = self.rr("sg", 2)
            self.tt(self.SG[:, gi, :], self.PS[pn][:], self.OT[:, sl], ALU.mult,
                    reads=[self.PSb[pn], self.OTb[n]] + ph, writes=[self.SGb[gi]])
            qi = self.rr("sq", 3)
            self.act(self.SQ[:, qi, :], self.SG[:, gi, :], AF.Square, reads=[self.SGb[gi]], writes=[self.SQb[qi]])
            pb = 2
            self.mm(self.PS[pb][:], CT[:, C_OM:C_OM + 128], self.SQ[:, qi, :], reads=[self.SQb[qi], self.constb],
                    writes=[self.PSb[pb]])
            ti = self.rr("t32", 2)
            ri = self.rr("rs", 2)
            self.act(self.T32[:, ti, :], self.PS[pq][:], AF.Square, reads=[self.PSb[pq]] + ph, writes=[self.T32b[ti]],
                     scale=1e-3)
            self.tt(self.T32[:, ti, :], self.T32[:, ti, :], self.ENB[:, ei, :], ALU.max,
                    reads=[self.T32b[ti], self.ENBb[ei]] + ph, writes=[self.T32b[ti]])
            self.tt(self.T32[:, ti, :], self.PS[pb][:], self.T32[:, ti, :], ALU.add,
                    reads=[self.PSb[pb], self.T32b[ti]], writes=[self.T32b[ti]])
            self.act(self.T32[:, ti, :], self.T32[:, ti, :], AF.Ln, reads=[self.T32b[ti]], writes=[self.T32b[ti]])
            self.act(self.RS[:, ri, :], self.T32[:, ti, :], AF.Exp, reads=[self.T32b[ti]], writes=[self.RSb[ri]],
                     scale=-0.5)
            self.stt(self.YTc[:, yi, sl], self.SG[:, gi, :], self.SM[:, 16 + hm:17 + hm], self.RS[:, ri, :],
                     ALU.mult, ALU.mult, reads=[self.SGb[gi], self.RSb[ri], self.constb] + ph, writes=[self.YTb[yi][n]])

        eis = {}
        eis[0] = span_rows(0)
        stage0(0)
        for c in range(16):
            n = c // 4
            if c % 4 == 0:
                pn, pq = ((6, 7), (0, 1))[n % 2]
                if n + 1 < NSP:
                    eis[n + 1] = span_rows(n + 1)
            if c + 1 < 16:
                stage0(c + 1)
            stage1(c)
            stage2(c, pn, pq)
            if c % 4 == 3:
                post(n, pn, pq, eis[n])


def _const_tables():
    ct = np.zeros((128, C_END), np.float32)
    k = np.arange(128)[:, None].astype(np.float64)
    q = np.arange(128)[None, :].astype(np.float64)
    for h in range(8):
        slope = 2.0 ** (-(h + 1))
        tabs = []
        for dil in (1, 4):
            cur = np.where(q >= k, np.exp(-slope * dil * np.maximum(q - k, 0)), 0.0)
            prev = np.where(k >= q, np.exp(-slope * dil * (q + 128 - k)), 0.0)
            tabs += [cur, prev]
        tabs.append(np.where(q >= k, np.exp(-slope * 16 * np.maximum(q - k, 0)), 0.0))
        for t, tab in enumerate(tabs):
            o = C_MT + (h * 5 + t) * 128
            ct[:, o:o + 128] = tab
    ct[:, C_ID:C_ID + 128] = np.eye(128)
    bd = np.zeros((128, 128))
    bd[:64, :64] = 1.0 / 64
    bd[64:, 64:] = 1.0 / 64
    ct[:, C_BD:C_BD + 128] = bd
    ct[:, C_CA:C_CA + 128] = (k <= q)
    ct[:, C_ON:C_ON + 128] = 1.0
    ct[:, C_OM:C_OM + 128] = 1.0 / 128
    ct[:64, C_WN:C_WN + 64] = 1.0 / 64
    ct[64, C_WN:C_WN + 64] = EPS
    c32 = np.zeros((128, 256), np.float32)
    c32[:, 0:128] = (k <= q)
    c32[:, 128:256] = 1.0
    return ct, c32


def _prep_shared(inp):
    f32 = np.float32
    A = lambda v: np.asarray(v, dtype=f32)
    sh = {}
    nw = np.stack([A(inp["ffn1_norm_w"])[0], A(inp["mix_norm_w"])[0], A(inp["ffn2_norm_w"])[0]], 0)
    sh["normw"] = np.ascontiguousarray(nw.reshape(3, NKC, 128).transpose(2, 0, 1).reshape(128, 3 * NKC))
    for f, pre in enumerate(("ffn1", "ffn2")):
        wg = A(inp[pre + "_w_gate"])[0].reshape(NKC, 128, NFC, 128)
        wu = A(inp[pre + "_w_up"])[0].reshape(NKC, 128, NFC, 128)
        gu = np.stack([wg, wu], 0)
        sh[f"wgu{f}"] = np.ascontiguousarray(gu.transpose(3, 2, 0, 1, 4).reshape(NFC, 128, 2048))
        sh[f"wd{f}"] = np.ascontiguousarray(A(inp[pre + "_w_down"])[0].reshape(NFC, 128, D))
    win = A(inp["w_in"])[0].reshape(NKC, 128, DIN)

    def cols2048(c_q, c_k):
        out = np.empty((4, 128, 2, NKC, 128), f32)
        for g in range(4):
            out[g, :, 0] = win[:, :, c_q + g * 128:c_q + g * 128 + 128].transpose(1, 0, 2)
            out[g, :, 1] = win[:, :, c_k + g * 128:c_k + g * 128 + 128].transpose(1, 0, 2)
        return np.ascontiguousarray(out.reshape(4, 128, 2048))

    def cols1024(c0):
        out = np.empty((4, 128, NKC, 128), f32)
        for g in range(4):
            out[g] = win[:, :, c0 + g * 128:c0 + g * 128 + 128].transpose(1, 0, 2)
        return np.ascontiguousarray(out.reshape(4, 128, 1024))

    sh["wqk_a"] = cols2048(0, 512)
    sh["wv_a"] = cols1024(1024)
    sh["wqk_m"] = cols2048(1536, 2048)
    sh["wv_m"] = cols1024(2560)
    sh["wo_m"] = cols1024(3072)
    sh["wg"] = np.ascontiguousarray(win[:, :, 3584:3592].transpose(1, 0, 2).reshape(128, 64))
    sh["wout"] = np.ascontiguousarray(A(inp["w_out"])[0].reshape(8, 128, 1024))
    sm = np.zeros((128, 80), f32)
    sm[:, 0] = np.tile(A(inp["q_norm_w"])[0], 2)
    sm[:, 1] = np.tile(A(inp["k_norm_w"])[0], 2)
    sm[:64, 8:16] = A(inp["attn_out_gain"])[0].reshape(8, 64).T
    sm[:, 16:20] = A(inp["mlstm_out_gain"])[0].reshape(4, 128).T
    sm[:, 20:24] = A(inp["i_bias"])[0][None, :]
    sm[:, 24:28] = A(inp["f_bias"])[0][None, :]
    sm[:, 28:36] = A(inp["conv_b"])[0].reshape(8, 128).T
    sm[:, 36:68] = A(inp["conv_w"])[0].reshape(4, 8, 128).transpose(2, 1, 0).reshape(128, 32)
    sh["small"] = sm
    ct, c32 = _const_tables()
    sh["ctab"] = ct
    sh["ctab32"] = c32
    return sh


_CACHE = {}


def _get_nc(stage, nseq):
    key = (stage, nseq)
    if key not in _CACHE:
        b = Builder(stage, nseq)
        _CACHE[key] = b.build()
        _CACHE[("tags",) + key] = {e: [o.tag for o in b.sc.ops if o.eng == e and not o.is_dma] for e in Sched.ENGS}
    return _CACHE[key]


def run(inp, stage="full", ncores=NCORES, nseq=2, trace=False):
    x = np.asarray(inp["x"], dtype=np.float32)
    sh = _prep_shared(inp)
    nc = _get_nc(stage, nseq)
    in_maps = []
    for c in range(ncores):
        xs = x[c * nseq:(c + 1) * nseq]
        xT = np.ascontiguousarray(xs.transpose(0, 2, 1).reshape(nseq, NKC, 128, S))
        m = dict(sh)
        m["xT"] = xT
        in_maps.append(m)
    res = run_bass_kernel_spmd(nc, in_maps, core_ids=list(range(ncores)), trace=trace)
    outs = []
    for c in range(ncores):
        oT = res.results[c]["outT"].reshape(nseq, D, S)
        outs.append(oT.transpose(0, 2, 1))
    out = np.ascontiguousarray(np.concatenate(outs, 0), dtype=np.float32)
    return out, res


def kernel(**inputs):
    out, _ = run(inputs, "full")
    return out
```

```python
import math
from contextlib import ExitStack

import numpy as np
import concourse.bass as bass
import concourse.mybir as mybir
from concourse.bass_utils import run_bass_kernel_spmd

F32 = mybir.dt.float32
BF16 = mybir.dt.bfloat16
AF = mybir.ActivationFunctionType
ALU = mybir.AluOpType

NCORES = 8
S = 2048
D = 1024
DFF = 2816
NFC = DFF // 128
NKC = D // 128
NSP = S // 512
DIN = 3592
EPS = 1e-6


class Buf:
    __slots__ = ("name", "w", "r", "excl")

    def __init__(self, name, excl=False):
        self.name = name
        self.w = None
        self.r = []
        self.excl = excl


class Op:
    __slots__ = ("eng", "fn", "deps", "odeps", "is_dma", "slot", "signal", "sem", "val", "tag", "cost", "lat", "aset", "idx")

    def __init__(self, eng, fn, is_dma, slot):
        self.eng = eng
        self.fn = fn
        self.is_dma = is_dma
        self.slot = slot
        self.deps = []
        self.odeps = []
        self.cost = 100.0
        self.lat = 0.0
        self.aset = None
        self.signal = is_dma
        self.sem = None
        self.val = 0


class Sched:
    ENGS = ("pe", "act", "dve", "pool", "sp")

    def __init__(self, nc):
        self.nc = nc
        self.ops = []
        self.final_waits = []
        self.tag = ""

    def op(self, eng, fn, reads=(), writes=(), dma=False, slot=None, cost=100.0, lat=0.0, aset=None):
        o = Op(eng, fn, dma, slot)
        o.tag = self.tag
        o.cost = cost
        o.lat = lat
        o.aset = aset
        deps = {}
        for b in reads:
            if b.w is not None:
                deps[id(b.w)] = (b.w, True)
            if b.excl:
                for r in b.r:
                    if r.eng != eng and id(r) not in deps:
                        deps[id(r)] = (r, False)
        for b in writes:
            if b.w is not None and id(b.w) not in deps:
                deps[id(b.w)] = (b.w, False)
            for r in b.r:
                if id(r) not in deps:
                    deps[id(r)] = (r, False)
        for d, raw in deps.values():
            if d is o:
                continue
            if (not d.is_dma) and (not dma) and d.eng == eng and not raw:
                o.odeps.append(d)
                continue
            o.deps.append(d)
        for b in reads:
            b.r.append(o)
        for b in writes:
            b.w = o
            b.r = []
        self.ops.append(o)
        return o

    def schedule(self, W=32, sem_lat=300.0):
        for i, o in enumerate(self.ops):
            o.idx = i
        rem = {e: [o for o in self.ops if o.eng == e] for e in self.ENGS}
        if W <= 1:
            return rem
        order = {e: [] for e in self.ENGS}
        self.trace_sim = []
        tfree = {e: 0.0 for e in self.ENGS}
        fin = {}
        issued = set()
        cur_set = [None]
        total = len(self.ops)
        while total:
            best = None
            for e in self.ENGS:
                lst = rem[e]
                tf = tfree[e]
                for k in range(min(W, len(lst))):
                    o = lst[k]
                    r = tf
                    ok = True
                    for d in o.deps:
                        f = fin.get(id(d))
                        if f is None:
                            ok = False
                            break
                        if f + sem_lat > r:
                            r = f + sem_lat
                    if not ok:
                        continue
                    for d in o.odeps:
                        if id(d) not in issued:
                            ok = False
                            break
                    if not ok:
                        continue
                    if o.aset is not None and cur_set[0] is not None and o.aset != cur_set[0]:
                        r += 1300.0
                    if best is None or (r, o.idx) < best[0]:
                        best = ((r, o.idx), e, k, o)
                    if r <= tf:
                        break
            assert best is not None
            (r, _), e, k, o = best
            rem[e].pop(k)
            order[e].append(o)
            issued.add(id(o))
            if o.aset is not None:
                cur_set[0] = o.aset
            self.trace_sim.append((e, o.tag, r, o.cost, tfree[e]))
            tfree[e] = r + o.cost
            fin[id(o)] = r + o.cost + o.lat
            total -= 1
        self.sim_time = max(tfree.values())
        return order

    def emit(self, stack, W=64):
        nc = self.nc
        streams = self.schedule(W)
        for o in self.ops:
            for d in o.deps:
                d.signal = True
        for o in self.final_waits:
            o.signal = True
        eng_sem = {e: stack.enter_context(nc.semaphore("sem_" + e)) for e in self.ENGS}
        cnt = {e: 0 for e in self.ENGS}
        slot_sem = {}
        slot_cnt = {}
        for o in [o for e in self.ENGS for o in streams[e]]:
            if o.is_dma:
                if o.slot not in slot_sem:
                    slot_sem[o.slot] = stack.enter_context(nc.semaphore("dq_" + o.slot))
                    slot_cnt[o.slot] = 0
                slot_cnt[o.slot] += 16
                o.sem = slot_sem[o.slot]
                o.val = slot_cnt[o.slot]
            elif o.signal:
                cnt[o.eng] += 1
                o.sem = eng_sem[o.eng]
                o.val = cnt[o.eng]
        self.n_sems = len(slot_sem) + len(eng_sem)
        finals = self.final_waits

        def run(e, eng):
            waited = {}
            for o in streams[e]:
                need = {}
                for d in o.deps:
                    k = id(d.sem)
                    if waited.get(k, 0) >= d.val:
                        continue
                    if k not in need or need[k][1] < d.val:
                        need[k] = (d.sem, d.val)
                for k, (sem, val) in need.items():
                    eng.wait_ge(sem, val)
                    waited[k] = val
                ins = o.fn(eng)
                if o.signal:
                    ins.then_inc(o.sem, 16 if o.is_dma else 1)
            if e == "sp":
                for o in finals:
                    if waited.get(id(o.sem), 0) < o.val:
                        eng.wait_ge(o.sem, o.val)
                        waited[id(o.sem)] = o.val

        with nc.Block() as block:
            @block.tensor
            def _(eng):
                run("pe", eng)

            @block.scalar
            def _(eng):
                run("act", eng)

            @block.vector
            def _(eng):
                run("dve", eng)

            @block.gpsimd
            def _(eng):
                run("pool", eng)

            @block.sync
            def _(eng):
                run("sp", eng)


GROUPS = (8, 7, 7)
NWD = 8
NWS = 3
ARENA = 55488
LN_QSCALE = math.log(0.125)
NMT = 8 * 5 * 128
C_MT = 0
C_ID = C_MT + NMT
C_BD = C_ID + 128
C_CA = C_BD + 128
C_ON = C_CA + 128
C_OM = C_ON + 128
C_WN = C_OM + 128
C_END = C_WN + 64


class Builder:
    def __init__(self, stage="full", nseq=2):
        self.stage = stage
        self.nseq = nseq
        self.nc = bass.Bass("TRN2", target_bir_lowering=False)
        self.stack = ExitStack()
        self.sc = Sched(self.nc)
        self.cnt = {}

    def sb(self, name, shape, dt):
        return self.stack.enter_context(self.nc.sbuf_tensor(name, list(shape), dt))

    def ps(self, name, shape, dt=F32):
        return self.stack.enter_context(self.nc.psum_tensor(name, list(shape), dt))

    def dram(self, name, shape, dt, kind="ExternalInput"):
        return self.nc.dram_tensor(name, list(shape), dt, kind=kind).ap()

    def buf(self, name):
        return Buf(name)

    def bufs(self, name, n):
        return [Buf(f"{name}{i}") for i in range(n)]

    def rr(self, key, n):
        v = self.cnt.get(key, 0)
        self.cnt[key] = v + 1
        return v % n

    def carve(self, nbytes):
        o = self.ar_off
        assert o % 4 == 0
        self.ar_off += (nbytes + 3) // 4 * 4
        assert self.ar_off <= ARENA, (self.ar_off, ARENA)
        return self.AR[:, o // 2:(o + nbytes) // 2]


    @staticmethod
    def fsz(ap):
        n = 1
        for d in ap.shape[1:]:
            n *= d
        return n

    def mm(self, out, lhsT, rhs, reads, writes, start=True, stop=True, skip=False):
        n = self.fsz(rhs)
        c = max(64, n) / 2.2 + 25.0
        if rhs.dtype == F32:
            c *= 4
        return self.sc.op("pe", lambda e: e.matmul(out, lhsT, rhs, start=start, stop=stop,
                                                   skip_group_check=skip), reads=reads, writes=writes, cost=c, lat=60.0)

    ASETS = {AF.Silu: "silu", AF.Sigmoid: "sig", AF.Exp: "lnexp", AF.Ln: "lnexp"}

    def act(self, out, in_, func, reads, writes, **kw):
        c = 230.0 + 0.84 * self.fsz(in_)
        return self.sc.op("act", lambda e: e.activation(out=out, in_=in_, func=func, **kw),
                          reads=reads, writes=writes, cost=c, lat=60.0, aset=self.ASETS.get(func))

    def stt(self, out, in0, scalar, in1, op0, op1, reads, writes):
        c = 120.0 + 1.05 * self.fsz(in0)
        return self.sc.op("dve", lambda e: e.scalar_tensor_tensor(out=out, in0=in0, scalar=scalar, in1=in1,
                                                                  op0=op0, op1=op1), reads=reads, writes=writes,
                          cost=c, lat=60.0)

    def tt(self, out, in0, in1, op, reads, writes, eng="dve"):
        n = self.fsz(in0)
        c = 120.0 + (0.55 * n if (in0.dtype == BF16 and in1.dtype == BF16) else 1.05 * n)
        if eng == "pool":
            c = 200.0 + 2.0 * n
        return self.sc.op(eng, lambda e: e.tensor_tensor(out=out, in0=in0, in1=in1, op=op),
                          reads=reads, writes=writes, cost=c, lat=60.0)

    def ts(self, out, in0, s1, s2, op0, op1, reads, writes, eng="dve"):
        c = 120.0 + 0.6 * self.fsz(in0)
        if op1 is None:
            return self.sc.op(eng, lambda e: e.tensor_scalar(out=out, in0=in0, scalar1=s1, scalar2=None, op0=op0),
                              reads=reads, writes=writes, cost=c, lat=60.0)
        return self.sc.op(eng, lambda e: e.tensor_scalar(out=out, in0=in0, scalar1=s1, scalar2=s2, op0=op0, op1=op1),
                          reads=reads, writes=writes, cost=c, lat=60.0)

    def cp(self, out, in_, reads, writes, eng="dve"):
        c = 120.0 + 1.05 * self.fsz(in_)
        return self.sc.op(eng, lambda e: e.tensor_copy(out=out, in_=in_), reads=reads, writes=writes, cost=c, lat=60.0)

    def dma(self, eng, out, in_, reads, writes, slot):
        nbytes = 128 * self.fsz(out) * 4
        return self.sc.op(eng, lambda e: e.dma_start(out=out, in_=in_), reads=reads, writes=writes, dma=True, slot=slot,
                          cost=600.0, lat=2500.0 + nbytes / 120.0)

    def build(self):
        nc, sc = self.nc, self.sc
        NS = self.nseq
        self.xT = self.dram("xT", [NS, NKC, 128, S], F32)
        self.outT = self.dram("outT", [NS, NKC, 128, S], F32, "ExternalOutput")
        self.normw = self.dram("normw", [128, 3 * NKC], F32)
        self.wgu = [self.dram(f"wgu{f}", [NFC, 128, 2048], F32) for f in range(2)]
        self.wd = [self.dram(f"wd{f}", [NFC, 128, D], F32) for f in range(2)]
        self.wqk_a = self.dram("wqk_a", [4, 128, 2048], F32)
        self.wv_a = self.dram("wv_a", [4, 128, 1024], F32)
        self.wqk_m = self.dram("wqk_m", [4, 128, 2048], F32)
        self.wv_m = self.dram("wv_m", [4, 128, 1024], F32)
        self.wo_m = self.dram("wo_m", [4, 128, 1024], F32)
        self.wg_d = self.dram("wg", [128, 64], F32)
        self.wout = self.dram("wout", [8, 128, 1024], F32)
        self.small_d = self.dram("small", [128, 80], F32)
        self.ctab_d = self.dram("ctab", [128, C_END], F32)
        self.ctab32_d = self.dram("ctab32", [128, 256], F32)

        self.XR = self.sb("XR", [128, NKC, S], F32)
        self.HT = self.sb("HT", [128, NKC, S], BF16)
        self.XRb = [self.bufs(f"XR{k}_", NSP) for k in range(NKC)]
        self.HTb = [self.bufs(f"HT{k}_", NSP) for k in range(NKC)]
        self.NW = self.sb("NW", [128, 3 * NKC], F32)
        self.SM = self.sb("SM", [128, 80], F32)
        self.CT = self.sb("CT", [128, C_END], BF16)
        self.CT32 = self.sb("CT32", [128, 256], F32)
        self.WG = self.sb("WG", [128, 64], BF16)
        self.ones_mean = self.sb("ones_mean", [128, 128], BF16)
        self.epsb = self.sb("epsb", [128, 1], F32)
        self.lnq = self.sb("lnq", [128, 1], F32)
        self.dummy = self.sb("dummyt", [128, 2], F32)
        self.joint = self.sb("joint", [128, 2], F32)
        self.constb = self.buf("consts")
        self.WGUs = self.sb("WGUs", [128, NWS, 2048], BF16)
        self.WGUb = self.bufs("WGU", NWS)
        self.WDs = self.sb("WDs", [128, NWD, D], BF16)
        self.WDb = self.bufs("WD", NWD)
        self.wgu_cnt = 0
        self.wd_cnt = 0
        self.SQ = self.sb("SQ", [128, 3, 512], BF16)
        self.SQb = self.bufs("SQ", 3)
        self.T32 = self.sb("T32", [128, 2, 512], F32)
        self.T32b = self.bufs("T32", 2)
        self.RS = self.sb("RS", [128, 2, 512], F32)
        self.RSb = self.bufs("RS", 2)
        self.SG = self.sb("SG", [128, 2, 512], F32)
        self.SGb = self.bufs("SG", 2)
        self.AR = self.sb("AR", [128, ARENA // 2], BF16)
        self.phase = self.buf("phase")
        self.PS = [self.ps(f"PS{i}", [128, 512]) for i in range(8)]
        self.PSb = [Buf(f"PS{i}", excl=True) for i in range(8)]

        self.ar_off = 0
        GM = max(GROUPS)
        self.ACTB = self.carve(GM * S * 2).rearrange("p (c t) -> p c t", c=GM)
        self.ACTb = [self.bufs(f"ACT{c}_", NSP) for c in range(GM)]
        self.ar_off = 0
        self.QT = self.carve(S * 2)
        self.KT = self.carve(S * 2)
        self.QTb = self.bufs("QT", NSP)
        self.KTb = self.bufs("KT", NSP)
        self.V = [self.carve(16 * 130 * 2).rearrange("p (m h c) -> p m h c", m=16, h=2) for _ in range(3)]
        self.Vb = [self.bufs(f"V{o}_", 16) for o in range(3)]
        self.NE = 6
        self.E = self.carve(self.NE * 512 * 2).rearrange("p (e t) -> p e t", e=self.NE)
        self.Eb = self.bufs("E", self.NE)
        attn_end = self.ar_off
        self.YTc = self.carve(4 * S * 2).rearrange("p (y t) -> p y t", y=4)
        self.YTb = [self.bufs(f"YT{y}_", NSP) for y in range(4)]
        self.mix_off = self.ar_off
        self.VT = self.carve(S * 2)
        self.VTb = self.bufs("VT", NSP)
        self.ar_off = 0
        self.XC = self.carve(2052 * 2)
        self.XCb = self.bufs("XC", NSP)
        self.QM = self.carve(S * 2)
        self.KM = self.carve(S * 2)
        self.QMb = self.bufs("QM", NSP)
        self.KMb = self.bufs("KM", NSP)
        self.DG = self.carve(2 * 4 * 128 * 2).rearrange("p (j t m) -> p j t m", j=2, t=4)
        self.DGb = self.bufs("DG", 2)
        self.VM = self.carve(16 * 128 * 2).rearrange("p (b m) -> p b m", b=16)
        self.VMb = self.bufs("VM", 16)
        self.OT = self.carve(S * 2)
        self.OTb = self.bufs("OT", NSP)
        self.PP = self.carve(2 * 128 * 2).rearrange("p (r m) -> p r m", r=2)
        self.PPb = self.bufs("PP", 2)
        self.KK = self.carve(2 * 128 * 2).rearrange("p (r m) -> p r m", r=2)
        self.KKb = self.bufs("KK", 2)
        self.CSf = self.carve(2 * 256 * 4).bitcast(F32).rearrange("p (r m) -> p r m", r=2)
        self.CSb = self.bufs("CS", 2)
        self.GT = self.carve(16 * 8 * 4).bitcast(F32).rearrange("p (b g) -> p b g", b=16)
        self.GTb = self.buf("GT")
        self.NLF = self.carve(16 * 4).bitcast(F32)
        self.NEGB = self.carve(16 * 4).bitcast(F32)
        self.EK = self.carve(16 * 4).bitcast(F32)
        self.EBL = self.carve(16 * 4).bitcast(F32)
        self.TMPA = self.carve(16 * 4).bitcast(F32)
        self.gb = {k: self.buf(k) for k in ("NLF", "NEGB", "EK", "TMPA")}
        self.EBLb = self.bufs("EBL", NSP)
        assert self.ar_off <= attn_end, (self.ar_off, attn_end)
        self.ar_off = self.mix_off
        self.RR = self.carve(2 * 512 * 4).bitcast(F32).rearrange("p (r m) -> p r m", r=2)
        self.RRb = self.bufs("RR", 2)
        self.ENB = self.carve(2 * 512 * 4).bitcast(F32).rearrange("p (r m) -> p r m", r=2)
        self.ENBb = self.bufs("ENB", 2)
        self.CB = self.carve(8 * 256 * 2).rearrange("p (c m) -> p c m", c=8)
        self.CBb = self.bufs("CB", 8)
        self.P3s = [self.PSb[3]] * 2
        self.P3t = [self.PSb[5]] * 2
        self.P4d = [self.PSb[4]] * 2
        self.PS3T = self.PS[5][:, 256:384].bitcast(BF16)
        self.fbn = self.sb("fbn", [128, 4], F32)
        self.ibk = self.sb("ibk", [128, 4], F32)
        self.oneb = self.sb("oneb", [128, 1], F32)
        self.lneps = self.sb("lneps", [128, 1], F32)

        self.msb = self.buf("ms")
        self.NWb = self.buf("NWb")
        for ap_, v in ((self.ones_mean[:], 1.0 / 1024.0), (self.epsb[:], EPS), (self.lnq[:], LN_QSCALE),
                       (self.oneb[:], 1.0), (self.lneps[:], math.log(EPS))):
            sc.op("pool", lambda e, ap_=ap_, v=v: e.memset(ap_, v), writes=[self.msb])
        self.dma("sp", self.NW[:], self.normw, [], [self.NWb], "c0")
        self.tables_done = False
        self.gate_consts_done = False

        full = self.stage in ("full", "ffn12")
        self.load_x(0)
        self.rmsnorm(0)
        for s in range(NS):
            if self.stage == "ffn1":
                self.ffn(s, 0)
            elif self.stage == "ffn12":
                self.ffn(s, 0, after_span=lambda n: self.rmsnorm_span(2, n))
            else:
                self.ffn(s, 0, after_span=lambda n: self.rmsnorm_span(1, n))
                self.mixer(s, after_span=(lambda n: self.rmsnorm_span(2, n)) if full else None)
            if full:
                def tail(n, s=s):
                    self.store_out(s, [n])
                    if s + 1 < NS:
                        self.load_x(s + 1, [n])
                        self.rmsnorm_span(0, n)
                self.ffn(s, 1, after_span=tail)
            else:
                self.store_out(s)
                if s + 1 < NS:
                    self.load_x(s + 1)
                    self.rmsnorm(0)
        sc.emit(self.stack)
        return nc

    def load_tables(self):
        sc = self.sc
        if self.tables_done:
            return
        self.tables_done = True
        tmpb = self.bufs("ctmp", 8)
        self.dma("sp", self.SM[:], self.small_d, [], [tmpb[0]], "c1")
        self.dma("sp", self.CT32[:], self.ctab32_d, [], [tmpb[1]], "c2")
        self.dma("pool", self.WG[:], self.wg_d, [], [tmpb[2]], "c3")
        for i, c0 in enumerate(range(0, C_END, 2048)):
            c1 = min(C_END, c0 + 2048)
            self.dma("pool", self.CT[:, c0:c1], self.ctab_d[:, c0:c1], [], [tmpb[3 + i]], f"ct{i}")
        sc.op("pool", lambda e: e.memset(self.joint[:], 0.0), reads=tmpb + [self.msb], writes=[self.constb])

    def barrier(self):
        self.sc.op("pool", lambda e: e.memset(self.dummy[:], 0.0), writes=[self.phase])

    def load_x(self, s, spans=range(NSP)):
        sc = self.sc
        for n in spans:
            for k in range(NKC):
                self.dma("sp", self.XR[:, k, n * 512:(n + 1) * 512], self.xT[s, k, :, n * 512:(n + 1) * 512],
                         [], [self.XRb[k][n]], f"x{k}_{n}")

    def store_out(self, s, spans=range(NSP)):
        sc = self.sc
        for n in spans:
            for k in range(NKC):
                o = self.dma("sp", self.outT[s, k, :, n * 512:(n + 1) * 512], self.XR[:, k, n * 512:(n + 1) * 512],
                             [self.XRb[k][n]], [], f"o{k}_{n}")
                sc.final_waits.append(o)

    def rmsnorm(self, widx):
        for n in range(NSP):
            self.rmsnorm_span(widx, n)

    def rmsnorm_span(self, widx, n):
        sc = self.sc
        sc.tag = "norm"
        if True:
            sl = slice(n * 512, (n + 1) * 512)
            pi = 6 + self.rr("nps", 2)
            for k in range(NKC):
                qi = self.rr("sq", 3)
                if k % 2 == 0:
                    self.act(self.SQ[:, qi, :], self.XR[:, k, sl], AF.Square, reads=[self.XRb[k][n]],
                             writes=[self.SQb[qi]])
                else:
                    self.tt(self.SQ[:, qi, :], self.XR[:, k, sl], self.XR[:, k, sl], ALU.mult,
                            reads=[self.XRb[k][n]], writes=[self.SQb[qi]])
                self.mm(self.PS[pi][:], self.ones_mean[:], self.SQ[:, qi, :], reads=[self.SQb[qi], self.msb],
                        writes=[self.PSb[pi]], start=(k == 0), stop=(k == NKC - 1))
            ti = self.rr("t32", 2)
            ri = self.rr("rs", 2)
            self.act(self.T32[:, ti, :], self.PS[pi][:], AF.Ln, reads=[self.PSb[pi], self.msb],
                     writes=[self.T32b[ti]], bias=self.epsb[:])
            self.act(self.RS[:, ri, :], self.T32[:, ti, :], AF.Exp, reads=[self.T32b[ti]], writes=[self.RSb[ri]],
                     scale=-0.5)
            for k in range(NKC):
                self.stt(self.HT[:, k, sl], self.XR[:, k, sl], self.NW[:, widx * NKC + k:widx * NKC + k + 1],
                         self.RS[:, ri, :], ALU.mult, ALU.mult,
                         reads=[self.XRb[k][n], self.RSb[ri], self.NWb], writes=[self.HTb[k][n]])

    def load_w2048(self, src):
        si = self.wgu_cnt % NWS
        self.wgu_cnt += 1
        self.dma("pool", self.WGUs[:, si, :], src, [], [self.WGUb[si]], f"wgu{si}")
        return si

    def load_w1024(self, src, ci=None):
        if ci is None:
            ci = self.wd_cnt % NWD
            self.wd_cnt += 1
        self.dma("pool", self.WDs[:, ci, :], src, [], [self.WDb[ci]], f"wd{ci}")
        return ci

    def ffn(self, s, f, after_span=None):
        sc = self.sc
        self.barrier()
        c0 = 0
        PRE = NWS - 1
        slots = {}
        for c in range(PRE):
            slots[c] = self.load_w2048(self.wgu[f][c])
        self.load_tables()
        for GS in GROUPS:
            chunks = list(range(c0, c0 + GS))
            c0 += GS
            for ci, c in enumerate(chunks):
                if c + PRE < NFC:
                    slots[c + PRE] = self.load_w2048(self.wgu[f][c + PRE])
                self.load_w1024(self.wd[f][c], ci)
                si = slots[c]
                sc.tag = "ffn.gu"
                for n in range(NSP):
                    sl = slice(n * 512, (n + 1) * 512)
                    pg = self.rr("pg", 2)
                    pu = 2 + self.rr("pu", 2)
                    for j, pi in ((0, pg), (1, pu)):
                        for k in range(NKC):
                            off = (j * NKC + k) * 128
                            self.mm(self.PS[pi][:], self.WGUs[:, si, off:off + 128], self.HT[:, k, sl],
                                    reads=[self.WGUb[si], self.HTb[k][n]], writes=[self.PSb[pi]],
                                    start=(k == 0), stop=(k == NKC - 1))
                    gi = self.rr("sg", 2)
                    self.act(self.SG[:, gi, :], self.PS[pg][:], AF.Silu, reads=[self.PSb[pg]], writes=[self.SGb[gi]])
                    self.tt(self.ACTB[:, ci, sl], self.PS[pu][:], self.SG[:, gi, :], ALU.mult,
                            reads=[self.PSb[pu], self.SGb[gi], self.phase], writes=[self.ACTb[ci][n]])
            last = (c0 == NFC)
            for n in range(NSP):
                sc.tag = "ffn.down"
                sl = slice(n * 512, (n + 1) * 512)
                for dc in range(NKC):
                    pd = 4 + self.rr("pd", 2)
                    for ci in range(GS):
                        self.mm(self.PS[pd][:], self.WDs[:, ci, dc * 128:(dc + 1) * 128], self.ACTB[:, ci, sl],
                                reads=[self.WDb[ci], self.ACTb[ci][n], self.phase], writes=[self.PSb[pd]],
                                start=(ci == 0), stop=(ci == GS - 1))
                    self.stt(self.XR[:, dc, sl], self.PS[pd][:], 0.5, self.XR[:, dc, sl], ALU.mult, ALU.add,
                             reads=[self.PSb[pd], self.XRb[dc][n]], writes=[self.XRb[dc][n]])
                if last and after_span is not None:
                    after_span(n)

    def mixer(self, s, after_span=None):
        sc = self.sc
        if not self.gate_consts_done:
            self.gate_consts_done = True
            self.ts(self.fbn[:], self.SM[:, 24:28], -1.0, None, ALU.mult, None, reads=[self.constb], writes=[self.constb])
            self.ts(self.ibk[:], self.SM[:, 20:24], -0.5 * math.log(128.0), None, ALU.add, None,
                    reads=[self.constb], writes=[self.constb])
        self.barrier()
        ph = [self.phase]
        for o in range(3):
            ap = self.V[o][:, :, :, 64:65]
            sc.op("pool", lambda e, ap=ap: e.memset(ap, 1.0), reads=ph, writes=self.Vb[o])
        for hp in range(4):
            self.attn_pair(s, hp)
        self.finish_group(0)
        if self.stage in ("full", "mix", "yT"):
            self.barrier()
            self.mlstm_gates()
            for hm in range(4):
                self.mlstm_head(s, hm)
            self.finish_group(1, after_span)

    def finish_group(self, g, after_span=None):
        if self.stage == "yT":
            for yi in range(4):
                for n in range(NSP):
                    sl = slice(n * 512, (n + 1) * 512)
                    self.cp(self.XR[:, 4 * g + yi, sl], self.YTc[:, yi, sl],
                            reads=[self.YTb[yi][n], self.phase], writes=[self.XRb[4 * g + yi][n]])
            return
        self.sc.tag = "wout"
        ois = [self.load_w1024(self.wout[4 * g + yi]) for yi in range(4)]
        for n in range(NSP):
            sl = slice(n * 512, (n + 1) * 512)
            for dc in range(NKC):
                pd = self.rr("pproj", 2)
                for yi in range(4):
                    self.mm(self.PS[pd][:], self.WDs[:, ois[yi], dc * 128:(dc + 1) * 128], self.YTc[:, yi, sl],
                            reads=[self.WDb[ois[yi]], self.YTb[yi][n], self.phase], writes=[self.PSb[pd]],
                            start=(yi == 0), stop=(yi == 3))
                self.tt(self.XR[:, dc, sl], self.PS[pd][:], self.XR[:, dc, sl], ALU.add,
                        reads=[self.PSb[pd], self.XRb[dc][n]], writes=[self.XRb[dc][n]])
            if after_span is not None:
                self.sc.tag = "norm"
                after_span(n)
                self.sc.tag = "wout"

    def attn_pair(self, s, hp):
        ph = [self.phase]
        CT = self.CT
        si = self.load_w2048(self.wqk_a[hp])
        vi = self.load_w1024(self.wv_a[hp])
        yi = hp
        self.sc.tag = "attn.qk"
        for j, dst, dstb in ((0, self.QT, self.QTb), (1, self.KT, self.KTb)):
            for n in range(NSP):
                sl = slice(n * 512, (n + 1) * 512)
                pa = (0, 1, 3, 4)[self.rr("pproj4", 4)]
                for k in range(NKC):
                    off = (j * NKC + k) * 128
                    self.mm(self.PS[pa][:], self.WGUs[:, si, off:off + 128], self.HT[:, k, sl],
                            reads=[self.WGUb[si], self.HTb[k][n]], writes=[self.PSb[pa]],
                            start=(k == 0), stop=(k == NKC - 1))
                qi = self.rr("sq", 3)
                self.act(self.SQ[:, qi, :], self.PS[pa][:], AF.Square, reads=[self.PSb[pa]], writes=[self.SQb[qi]])
                pb = 2
                self.mm(self.PS[pb][:], CT[:, C_BD:C_BD + 128], self.SQ[:, qi, :],
                        reads=[self.SQb[qi], self.constb], writes=[self.PSb[pb]])
                ti = self.rr("t32", 2)
                ri = self.rr("rs", 2)
                self.act(self.T32[:, ti, :], self.PS[pb][:], AF.Ln, reads=[self.PSb[pb], self.constb],
                         writes=[self.T32b[ti]], bias=self.epsb[:])
                if j == 0:
                    self.act(self.RS[:, ri, :], self.T32[:, ti, :], AF.Exp, reads=[self.T32b[ti], self.constb],
                             writes=[self.RSb[ri]], scale=-0.5, bias=self.lnq[:])
                else:
                    self.act(self.RS[:, ri, :], self.T32[:, ti, :], AF.Exp, reads=[self.T32b[ti]],
                             writes=[self.RSb[ri]], scale=-0.5)
                self.stt(dst[:, sl], self.PS[pa][:], self.SM[:, j:j + 1], self.RS[:, ri, :], ALU.mult, ALU.mult,
                         reads=[self.PSb[pa], self.RSb[ri], self.constb] + ph, writes=[dstb[n]])

        self.sc.tag = "attn.v"
        def tokcols(o, m):
            if o == 0:
                return slice(128 * m, 128 * m + 128), [m // 4]
            if o == 1:
                n, r = divmod(m, 4)
                return slice(512 * n + r, 512 * n + 512, 4), [n]
            return slice(m, S, 16), [0, 1, 2, 3]

        ident = CT[:, C_ID:C_ID + 128]
        for n in range(NSP):
            sl = slice(n * 512, (n + 1) * 512)
            pa = (0, 1, 3, 4)[self.rr("pproj4", 4)]
            for k in range(NKC):
                self.mm(self.PS[pa][:], self.WDs[:, vi, k * 128:(k + 1) * 128], self.HT[:, k, sl],
                        reads=[self.WDb[vi], self.HTb[k][n]], writes=[self.PSb[pa]],
                        start=(k == 0), stop=(k == NKC - 1))
            self.act(self.VT[:, sl], self.PS[pa][:], AF.Copy, reads=[self.PSb[pa]] + ph, writes=[self.VTb[n]])
        for o in range(3):
            for m0 in range(0, 16, 4):
                pv = self.rr("pproj", 2)
                pvb = self.PS[pv][:, 0:256].bitcast(BF16)
                for mm_ in range(4):
                    cols, spans = tokcols(o, m0 + mm_)
                    outap = pvb[:, mm_ * 128:(mm_ + 1) * 128]
                    inap = self.VT[:, cols]
                    self.sc.op("pe", lambda e, outap=outap, inap=inap: e.transpose(outap, inap, ident),
                               reads=[self.VTb[n] for n in spans] + [self.constb] + ph, writes=[self.PSb[pv]],
                               cost=90.0, lat=60.0)
                self.act(self.V[o][:, m0:m0 + 4, :, 0:64],
                         pvb.rearrange("p (a b c) -> p a b c", a=4, b=2, c=64), AF.Copy,
                         reads=[self.PSb[pv]] + ph, writes=[self.Vb[o][m0 + i] for i in range(4)])

        for hh in range(2):
            h = 2 * hp + hh
            rows = slice(hh * 64, hh * 64 + 64)
            for n in range(NSP):
                self.attn_unit(h, hh, rows, n, yi)

    def attn_unit(self, h, hh, rows, n, yi):
        ph = [self.phase]
        CT = self.CT
        sl = slice(n * 512, (n + 1) * 512)
        acc = 6 + self.rr("acc", 2)
        state = {"first": True}

        def mt(t):
            return C_MT + (h * 5 + t) * 128

        def pv_mm(vo, blk, ei, ecols, ocols):
            st = state["first"]
            state["first"] = False
            self.mm(self.PS[acc][0:65, ocols], self.V[vo][:, blk, hh, 0:65], self.E[:, ei, ecols],
                    reads=[self.Vb[vo][blk], self.Eb[ei]] + ph, writes=[self.PSb[acc]],
                    start=st, stop=False, skip=True)

        def exp_mask(pi, ei, c0, mask_ap, a):
            self.act(self.E[:, ei, c0:512], self.PS[pi][:, c0:512], AF.Exp,
                     reads=[self.PSb[pi]] + ph, writes=[self.Eb[ei]])
            ev = self.E[:, ei, c0:512].rearrange("p (a b) -> p a b", a=a)
            self.tt(ev, ev, mask_ap, ALU.mult, reads=[self.Eb[ei], self.constb] + ph, writes=[self.Eb[ei]])

        for o, tcur, tprev in ((0, 0, 1), (1, 2, 3)):
            for prev in (0, 1):
                self.sc.tag = "attn.d1" if o == 0 else "attn.d4"
                if o == 1 and prev and n == 0:
                    continue
                j0 = 1 if (o == 0 and prev and n == 0) else 0
                pi = 3 + self.rr("psc", 3)
                ei = self.rr("e", self.NE)
                kbs = {}
                for jq in range(j0, 4):
                    if o == 0:
                        qb = 4 * n + jq
                        kb = qb - prev
                        qc = slice(128 * qb, 128 * qb + 128)
                        kc_ = slice(128 * kb, 128 * kb + 128)
                        kspan = kb // 4
                    else:
                        kn = n - prev
                        kb = 4 * kn + jq
                        qc = slice(512 * n + jq, 512 * n + 512, 4)
                        kc_ = slice(512 * kn + jq, 512 * kn + 512, 4)
                        kspan = kn
                    kbs[jq] = kb
                    self.mm(self.PS[pi][:, jq * 128:(jq + 1) * 128], self.KT[rows, kc_], self.QT[rows, qc],
                            reads=[self.KTb[kspan], self.QTb[n]] + ph, writes=[self.PSb[pi]])
                t = tprev if prev else tcur
                nb = 4 - j0
                mask_ap = CT[:, mt(t):mt(t) + 128].unsqueeze(1).to_broadcast([128, nb, 128])
                exp_mask(pi, ei, j0 * 128, mask_ap, nb)
                for jq in range(j0, 4):
                    ocols = slice(jq * 128, jq * 128 + 128) if o == 0 else slice(jq, 512, 4)
                    pv_mm(o, kbs[jq], ei, slice(jq * 128, jq * 128 + 128), ocols)
        self.sc.tag = "attn.d16"
        pi = 3 + self.rr("psc", 3)
        ei = self.rr("e", self.NE)
        for r in range(16):
            self.mm(self.PS[pi][:, r * 32:(r + 1) * 32], self.KT[rows, slice(r, S, 16)],
                    self.QT[rows, slice(512 * n + r, 512 * n + 512, 16)],
                    reads=self.KTb + [self.QTb[n]] + ph, writes=[self.PSb[pi]])
        mask_ap = CT[:, mt(4) + 32 * n:mt(4) + 32 * n + 32].unsqueeze(1).to_broadcast([128, 16, 32])
        exp_mask(pi, ei, 0, mask_ap, 16)
        for r in range(16):
            pv_mm(2, r, ei, slice(r * 32, r * 32 + 32), slice(r, 512, 16))
        self.sc.tag = "attn.post"
        qi = self.rr("sq", 3)
        self.act(self.SQ[0:65, qi, :], self.PS[acc][0:65, :], AF.Square, reads=[self.PSb[acc]], writes=[self.SQb[qi]])
        pb = 2
        self.mm(self.PS[pb][0:64, :], CT[0:65, C_WN:C_WN + 64], self.SQ[0:65, qi, :],
                reads=[self.SQb[qi], self.constb], writes=[self.PSb[pb]])
        ti = self.rr("t32", 2)
        ri = self.rr("rs", 2)
        self.act(self.T32[0:64, ti, :], self.PS[pb][0:64, :], AF.Ln, reads=[self.PSb[pb]], writes=[self.T32b[ti]])
        self.act(self.RS[0:64, ri, :], self.T32[0:64, ti, :], AF.Exp, reads=[self.T32b[ti]], writes=[self.RSb[ri]],
                 scale=-0.5)
        self.stt(self.YTc[rows, yi, sl], self.PS[acc][0:64, :], self.SM[0:64, 8 + h:9 + h], self.RS[0:64, ri, :],
                 ALU.mult, ALU.mult, reads=[self.PSb[acc], self.RSb[ri], self.constb] + ph, writes=[self.YTb[yi][n]])

    def mlstm_gates(self):
        ph = [self.phase]
        self.sc.tag = "ml.gates"
        for blk in range(16):
            cols = slice(128 * blk, 128 * blk + 128)
            for k in range(NKC):
                self.mm(self.PS[5][:, blk * 8:(blk + 1) * 8], self.HT[:, k, cols], self.WG[:, k * 8:(k + 1) * 8],
                        reads=[self.HTb[k][blk // 4], self.constb] + ph, writes=[self.PSb[5]],
                        start=(k == 0), stop=(k == NKC - 1))
        self.cp(self.GT[:], self.PS[5][:, 0:128].rearrange("p (b g) -> p b g", b=16),
                reads=[self.PSb[5]] + ph, writes=[self.GTb])
        ap = self.XC[:, 0:3]
        self.sc.op("pool", lambda e: e.memset(ap, 0.0), reads=ph, writes=[self.XCb[0]])

    def mlstm_head(self, s, hm):
        ph = [self.phase]
        CT, CT32, gb = self.CT, self.CT32, self.gb
        si = self.load_w2048(self.wqk_m[hm])
        vi = self.load_w1024(self.wv_m[hm])
        oi = self.load_w1024(self.wo_m[hm])
        yi = hm
        ident = CT[:, C_ID:C_ID + 128]
        ones_bf = CT[:, C_ON:C_ON + 128]
        self.sc.tag = "ml.qkconv"
        for j, dst, dstb in ((0, self.QM, self.QMb), (1, self.KM, self.KMb)):
            c8 = j * 4 + hm
            wcol = lambda tap: self.SM[:, 36 + c8 * 4 + tap:37 + c8 * 4 + tap]
            for n in range(NSP):
                sl = slice(n * 512, (n + 1) * 512)
                pa = (0, 1, 3, 4)[self.rr("pproj4", 4)]
                for k in range(NKC):
                    off = (j * NKC + k) * 128
                    self.mm(self.PS[pa][:], self.WGUs[:, si, off:off + 128], self.HT[:, k, sl],
                            reads=[self.WGUb[si], self.HTb[k][n]], writes=[self.PSb[pa]],
                            start=(k == 0), stop=(k == NKC - 1))
                self.act(self.XC[:, 3 + 512 * n:3 + 512 * n + 512], self.PS[pa][:], AF.Copy,
                         reads=[self.PSb[pa]] + ph, writes=[self.XCb[n]])
                rd = [self.XCb[n]] + ([self.XCb[n - 1]] if n > 0 else [])
                gi = self.rr("sg", 2)
                self.ts(self.SG[:, gi, :], self.XC[:, 512 * n:512 * n + 512], wcol(0), self.SM[:, 28 + c8:29 + c8],
                        ALU.mult, ALU.add, reads=rd + [self.constb] + ph, writes=[self.SGb[gi]])
                for tap in (1, 2, 3):
                    self.stt(self.SG[:, gi, :], self.XC[:, 512 * n + tap:512 * n + tap + 512], wcol(tap),
                             self.SG[:, gi, :], ALU.mult, ALU.add, reads=rd + [self.SGb[gi], self.constb] + ph,
                             writes=[self.SGb[gi]])
                self.act(dst[:, sl], self.SG[:, gi, :], AF.Silu, reads=[self.SGb[gi]] + ph, writes=[dstb[n]])
        self.sc.tag = "ml.vo"
        for m0 in range(0, 16, 4):
            pv = self.rr("pproj", 2)
            for mm_ in range(4):
                blk = m0 + mm_
                cols = slice(128 * blk, 128 * blk + 128)
                for k in range(NKC):
                    self.mm(self.PS[pv][:, mm_ * 128:(mm_ + 1) * 128], self.HT[:, k, cols],
                            self.WDs[:, vi, k * 128:(k + 1) * 128],
                            reads=[self.WDb[vi], self.HTb[k][blk // 4]], writes=[self.PSb[pv]],
                            start=(k == 0), stop=(k == NKC - 1))
            self.act(self.VM[:, m0:m0 + 4, :], self.PS[pv][:, 0:512].rearrange("p (a b) -> p a b", a=4), AF.Copy,
                     reads=[self.PSb[pv]] + ph, writes=[self.VMb[m0 + i] for i in range(4)])
        for n in range(NSP):
            sl = slice(n * 512, (n + 1) * 512)
            pa = self.rr("pproj", 2)
            for k in range(NKC):
                self.mm(self.PS[pa][:], self.WDs[:, oi, k * 128:(k + 1) * 128], self.HT[:, k, sl],
                        reads=[self.WDb[oi], self.HTb[k][n]], writes=[self.PSb[pa]],
                        start=(k == 0), stop=(k == NKC - 1))
            self.act(self.OT[:, sl], self.PS[pa][:], AF.Sigmoid, reads=[self.PSb[pa]] + ph, writes=[self.OTb[n]])
        self.sc.tag = "ml.gate2"
        self.act(self.TMPA[:], self.GT[:, :, 4 + hm], AF.Exp, reads=[self.GTb, self.constb] + ph, writes=[gb["TMPA"]],
                 scale=-1.0, bias=self.fbn[:, hm:hm + 1])
        self.act(self.NLF[:], self.TMPA[:], AF.Ln, reads=[gb["TMPA"], self.constb] + ph, writes=[gb["NLF"]],
                 bias=self.oneb[:])
        self.mm(self.PS[5][:, 128:144], CT32[:, 0:128], self.NLF[:], reads=[gb["NLF"], self.constb] + ph,
                writes=[self.PSb[5]])
        self.tt(self.NEGB[:], self.PS[5][:, 128:144], self.GT[:, :, hm], ALU.add,
                reads=[self.PSb[5], self.GTb] + ph, writes=[gb["NEGB"]])
        self.act(self.EK[:], self.NEGB[:], AF.Exp, reads=[gb["NEGB"], self.constb] + ph, writes=[gb["EK"]],
                 bias=self.ibk[:, hm:hm + 1])

        def span_rows(n):
            self.sc.tag = "ml.rows"
            ri = self.rr("rr", 2)
            self.tt(self.RR[:, ri, :].rearrange("p (a b) -> p a b", a=4),
                    CT32[:, 0:128].unsqueeze(1).to_broadcast([128, 4, 128]),
                    self.NLF[:, 4 * n:4 * n + 4].unsqueeze(2).to_broadcast([128, 4, 128]), ALU.mult,
                    reads=[gb["NLF"], self.constb] + ph, writes=[self.RRb[ri]])
            self.mm(self.PS[5][:], CT32[:, 128:256], self.RR[:, ri, :], reads=[self.RRb[ri], self.constb] + ph,
                    writes=[self.PSb[5]])
            ei = self.rr("enb", 2)
            self.act(self.ENB[:, ei, :], self.PS[5][:], AF.Exp, reads=[self.PSb[5], self.constb] + ph,
                     writes=[self.ENBb[ei]], scale=2.0, bias=self.lneps[:])
            self.act(self.EBL[:, 4 * n:4 * n + 4], self.PS[5][:, 127:512:128], AF.Exp,
                     reads=[self.PSb[5]] + ph, writes=[self.EBLb[n]], scale=-1.0)
            return ei

        def stage0(c):
            self.sc.tag = "ml.s0"
            cols = slice(128 * c, 128 * c + 128)
            r = c % 2
            self.mm(self.PS[3][:, r * 128:(r + 1) * 128], self.KM[:, cols], self.QM[:, cols],
                    reads=[self.KMb[c // 4], self.QMb[c // 4]] + ph, writes=[self.P3s[r]])
            self.stt(self.PP[:, r, :], self.PS[3][:, r * 128:(r + 1) * 128], self.EK[:, c:c + 1],
                     CT[:, C_CA:C_CA + 128], ALU.mult, ALU.mult,
                     reads=[self.P3s[r], gb["EK"], self.constb] + ph, writes=[self.PPb[r]])
            tp = self.PS3T[:, r * 128:(r + 1) * 128]
            self.sc.op("pe", lambda e: e.transpose(tp, self.KM[:, cols], ident),
                       reads=[self.KMb[c // 4], self.constb] + ph, writes=[self.P3t[r]], cost=90.0, lat=60.0)
            self.act(self.KK[:, r, :], tp, AF.Copy, reads=[self.P3t[r], gb["EK"]] + ph, writes=[self.KKb[r]],
                     scale=self.EK[:, c:c + 1])

        def stage1(c):
            self.sc.tag = "ml.s1"
            r = c % 2
            d0 = r * 256
            self.mm(self.PS[4][:, d0:d0 + 128], self.KK[:, r, :], self.VM[:, c, :],
                    reads=[self.KKb[r], self.VMb[c]] + ph, writes=[self.P4d[r]])
            self.mm(self.PS[4][:, d0 + 128:d0 + 256], self.KK[:, r, :], ones_bf,
                    reads=[self.KKb[r], self.constb] + ph, writes=[self.P4d[r]])
            if c == 0:
                self.cp(self.CSf[:, r, :], self.PS[4][:, d0:d0 + 256], reads=[self.P4d[r]] + ph, writes=[self.CSb[r]])
            else:
                self.stt(self.CSf[:, r, :], self.CSf[:, 1 - r, :], self.EBL[:, c - 1:c], self.PS[4][:, d0:d0 + 256],
                         ALU.mult, ALU.add, reads=[self.CSb[1 - r], self.EBLb[(c - 1) // 4], self.P4d[r]] + ph,
                         writes=[self.CSb[r]])
            if c < 15:
                self.act(self.CB[:, c % 8, :], self.CSf[:, r, :], AF.Copy, reads=[self.CSb[r], self.EBLb[c // 4]] + ph,
                         writes=[self.CBb[c % 8]], scale=self.EBL[:, c:c + 1])

        def stage2(c, pn, pq):
            self.sc.tag = "ml.s2"
            cols = slice(128 * c, 128 * c + 128)
            r = c % 2
            cc = c % 4
            osl = slice(cc * 128, cc * 128 + 128)
            self.mm(self.PS[pn][:, osl], self.VM[:, c, :], self.PP[:, r, :],
                    reads=[self.VMb[c], self.PPb[r]] + ph, writes=[self.PSb[pn]], start=True, stop=(c == 0), skip=True)
            if c > 0:
                self.mm(self.PS[pn][:, osl], self.CB[:, (c - 1) % 8, 0:128], self.QM[:, cols],
                        reads=[self.CBb[(c - 1) % 8], self.QMb[c // 4]] + ph, writes=[self.PSb[pn]],
                        start=False, stop=True, skip=True)
            self.mm(self.PS[pq][:, osl], ones_bf, self.PP[:, r, :],
                    reads=[self.PPb[r], self.constb] + ph, writes=[self.PSb[pq]], start=True, stop=(c == 0), skip=True)
            if c > 0:
                self.mm(self.PS[pq][:, osl], self.CB[:, (c - 1) % 8, 128:256], self.QM[:, cols],
                        reads=[self.CBb[(c - 1) % 8], self.QMb[c // 4]] + ph, writes=[self.PSb[pq]],
                        start=False, stop=True, skip=True)

        def post(n, pn, pq, ei):
            self.sc.tag = "ml.post"
            sl = slice(n * 512, (n + 1) * 512)
            gi = self.rr("sg", 2)
            self.tt(self.SG[:, gi, :], self.PS[pn][:], self.OT[:, sl], ALU.mult,
                    reads=[self.PSb[pn], self.OTb[n]] + ph, writes=[self.SGb[gi]])
            qi = self.rr("sq", 3)
            self.act(self.SQ[:, qi, :], self.SG[:, gi, :], AF.Square, reads=[self.SGb[gi]], writes=[self.SQb[qi]])
            pb = 2
            self.mm(self.PS[pb][:], CT[:, C_OM:C_OM + 128], self.SQ[:, qi, :], reads=[self.SQb[qi], self.constb],
                    writes=[self.PSb[pb]])
            ti = self.rr("t32", 2)
            ri = self.rr("rs", 2)
            self.act(self.T32[:, ti, :], self.PS[pq][:], AF.Square, reads=[self.PSb[pq]] + ph, writes=[self.T32b[ti]],
                     scale=1e-3)
            self.tt(self.T32[:, ti, :], self.T32[:, ti, :], self.ENB[:, ei, :], ALU.max,
                    reads=[self.T32b[ti], self.ENBb[ei]] + ph, writes=[self.T32b[ti]])
            self.tt(self.T32[:, ti, :], self.PS[pb][:], self.T32[:, ti, :], ALU.add,
                    reads=[self.PSb[pb], self.T32b[ti]], writes=[self.T32b[ti]])
            self.act(self.T32[:, ti, :], self.T32[:, ti, :], AF.Ln, reads=[self.T32b[ti]], writes=[self.T32b[ti]])
            self.act(self.RS[:, ri, :], self.T32[:, ti, :], AF.Exp, reads=[self.T32b[ti]], writes=[self.RSb[ri]],
                     scale=-0.5)
            self.stt(self.YTc[:, yi, sl], self.SG[:, gi, :], self.SM[:, 16 + hm:17 + hm], self.RS[:, ri, :],
                     ALU.mult, ALU.mult, reads=[self.SGb[gi], self.RSb[ri], self.constb] + ph, writes=[self.YTb[yi][n]])

        eis = {}
        eis[0] = span_rows(0)
        stage0(0)
        for c in range(16):
            n = c // 4
            if c % 4 == 0:
                pn, pq = ((6, 7), (0, 1))[n % 2]
                if n + 1 < NSP:
                    eis[n + 1] = span_rows(n + 1)
            if c + 1 < 16:
                stage0(c + 1)
            stage1(c)
            stage2(c, pn, pq)
            if c % 4 == 3:
                post(n, pn, pq, eis[n])


def _const_tables():
    ct = np.zeros((128, C_END), np.float32)
    k = np.arange(128)[:, None].astype(np.float64)
    q = np.arange(128)[None, :].astype(np.float64)
    for h in range(8):
        slope = 2.0 ** (-(h + 1))
        tabs = []
        for dil in (1, 4):
            cur = np.where(q >= k, np.exp(-slope * dil * np.maximum(q - k, 0)), 0.0)
            prev = np.where(k >= q, np.exp(-slope * dil * (q + 128 - k)), 0.0)
            tabs += [cur, prev]
        tabs.append(np.where(q >= k, np.exp(-slope * 16 * np.maximum(q - k, 0)), 0.0))
        for t, tab in enumerate(tabs):
            o = C_MT + (h * 5 + t) * 128
            ct[:, o:o + 128] = tab
    ct[:, C_ID:C_ID + 128] = np.eye(128)
    bd = np.zeros((128, 128))
    bd[:64, :64] = 1.0 / 64
    bd[64:, 64:] = 1.0 / 64
    ct[:, C_BD:C_BD + 128] = bd
    ct[:, C_CA:C_CA + 128] = (k <= q)
    ct[:, C_ON:C_ON + 128] = 1.0
    ct[:, C_OM:C_OM + 128] = 1.0 / 128
    ct[:64, C_WN:C_WN + 64] = 1.0 / 64
    ct[64, C_WN:C_WN + 64] = EPS
    c32 = np.zeros((128, 256), np.float32)
    c32[:, 0:128] = (k <= q)
    c32[:, 128:256] = 1.0
    return ct, c32


def _prep_shared(inp):
    f32 = np.float32
    A = lambda v: np.asarray(v, dtype=f32)
    sh = {}
    nw = np.stack([A(inp["ffn1_norm_w"])[0], A(inp["mix_norm_w"])[0], A(inp["ffn2_norm_w"])[0]], 0)
    sh["normw"] = np.ascontiguousarray(nw.reshape(3, NKC, 128).transpose(2, 0, 1).reshape(128, 3 * NKC))
    for f, pre in enumerate(("ffn1", "ffn2")):
        wg = A(inp[pre + "_w_gate"])[0].reshape(NKC, 128, NFC, 128)
        wu = A(inp[pre + "_w_up"])[0].reshape(NKC, 128, NFC, 128)
        gu = np.stack([wg, wu], 0)
        sh[f"wgu{f}"] = np.ascontiguousarray(gu.transpose(3, 2, 0, 1, 4).reshape(NFC, 128, 2048))
        sh[f"wd{f}"] = np.ascontiguousarray(A(inp[pre + "_w_down"])[0].reshape(NFC, 128, D))
    win = A(inp["w_in"])[0].reshape(NKC, 128, DIN)

    def cols2048(c_q, c_k):
        out = np.empty((4, 128, 2, NKC, 128), f32)
        for g in range(4):
            out[g, :, 0] = win[:, :, c_q + g * 128:c_q + g * 128 + 128].transpose(1, 0, 2)
            out[g, :, 1] = win[:, :, c_k + g * 128:c_k + g * 128 + 128].transpose(1, 0, 2)
        return np.ascontiguousarray(out.reshape(4, 128, 2048))

    def cols1024(c0):
        out = np.empty((4, 128, NKC, 128), f32)
        for g in range(4):
            out[g] = win[:, :, c0 + g * 128:c0 + g * 128 + 128].transpose(1, 0, 2)
        return np.ascontiguousarray(out.reshape(4, 128, 1024))

    sh["wqk_a"] = cols2048(0, 512)
    sh["wv_a"] = cols1024(1024)
    sh["wqk_m"] = cols2048(1536, 2048)
    sh["wv_m"] = cols1024(2560)
    sh["wo_m"] = cols1024(3072)
    sh["wg"] = np.ascontiguousarray(win[:, :, 3584:3592].transpose(1, 0, 2).reshape(128, 64))
    sh["wout"] = np.ascontiguousarray(A(inp["w_out"])[0].reshape(8, 128, 1024))
    sm = np.zeros((128, 80), f32)
    sm[:, 0] = np.tile(A(inp["q_norm_w"])[0], 2)
    sm[:, 1] = np.tile(A(inp["k_norm_w"])[0], 2)
    sm[:64, 8:16] = A(inp["attn_out_gain"])[0].reshape(8, 64).T
    sm[:, 16:20] = A(inp["mlstm_out_gain"])[0].reshape(4, 128).T
    sm[:, 20:24] = A(inp["i_bias"])[0][None, :]
    sm[:, 24:28] = A(inp["f_bias"])[0][None, :]
    sm[:, 28:36] = A(inp["conv_b"])[0].reshape(8, 128).T
    sm[:, 36:68] = A(inp["conv_w"])[0].reshape(4, 8, 128).transpose(2, 1, 0).reshape(128, 32)
    sh["small"] = sm
    ct, c32 = _const_tables()
    sh["ctab"] = ct
    sh["ctab32"] = c32
    return sh


_CACHE = {}


def _get_nc(stage, nseq):
    key = (stage, nseq)
    if key not in _CACHE:
        b = Builder(stage, nseq)
        _CACHE[key] = b.build()
        _CACHE[("tags",) + key] = {e: [o.tag for o in b.sc.ops if o.eng == e and not o.is_dma] for e in Sched.ENGS}
    return _CACHE[key]


def run(inp, stage="full", ncores=NCORES, nseq=2, trace=False):
    x = np.asarray(inp["x"], dtype=np.float32)
    sh = _prep_shared(inp)
    nc = _get_nc(stage, nseq)
    in_maps = []
    for c in range(ncores):
        xs = x[c * nseq:(c + 1) * nseq]
        xT = np.ascontiguousarray(xs.transpose(0, 2, 1).reshape(nseq, NKC, 128, S))
        m = dict(sh)
        m["xT"] = xT
        in_maps.append(m)
    res = run_bass_kernel_spmd(nc, in_maps, core_ids=list(range(ncores)), trace=trace)
    outs = []
    for c in range(ncores):
        oT = res.results[c]["outT"].reshape(nseq, D, S)
        outs.append(oT.transpose(0, 2, 1))
    out = np.ascontiguousarray(np.concatenate(outs, 0), dtype=np.float32)
    return out, res


def kernel(**inputs):
    out, _ = run(inputs, "full")
    return out
```
